# Optimizing a Trainium2 kernel written in Bass

```python
import jax, jax.numpy as jnp
from jax import lax
import numpy as np

D_MODEL = 2048
BATCH = 16
SEQ = 2048
DEPTH = 4

MEM_LEN = 256
D_A = D_MODEL // 2
D_B = D_MODEL - D_A
SGU_CHUNK = 128
SGU_GROUPS = 8
SGU_GROUP_DIM = D_A // SGU_GROUPS
HGRN_HEAD_DIM = 128
HGRN_HEADS = D_B // HGRN_HEAD_DIM
HGRN_CHUNK = 64
EVEN_IN = 2 * D_A + 4 * D_B
CONV_WIDTH = 3
XA_HEADS = 4
XA_HEAD_DIM = D_MODEL // XA_HEADS
PEER_HEADS = 8
PEER_NKEYS = 128
PEER_EXPERTS = PEER_NKEYS * PEER_NKEYS
PEER_QDIM = 256
PEER_HALF = PEER_QDIM // 2
PEER_TOPK = 16
PEER_TOKEN_BLOCK = 128
N_EVEN = (DEPTH + 1) // 2
N_ODD = DEPTH // 2
DN_ALPHA = (2.0 * DEPTH) ** 0.25
DN_BETA = (8.0 * DEPTH) ** -0.25
LN_EPS = 1e-5
F_FLOOR = 1e-30

kernel_name = "hybrid_sgu_hgrn2_shortconv_peer_deepnorm"


def layer_norm(x, g, b):
    xf = x.astype(jnp.float32)
    mu = jnp.mean(xf, -1, keepdims=True)
    var = jnp.mean(jnp.square(xf - mu), -1, keepdims=True)
    return ((xf - mu) * lax.rsqrt(var + LN_EPS) * g + b).astype(x.dtype)


def rms_norm(x, g):
    xf = x.astype(jnp.float32)
    return (xf * lax.rsqrt(jnp.mean(xf * xf, -1, keepdims=True) + LN_EPS) * g).astype(x.dtype)


def spatial_gating(u, v, ln_g, ln_b, w_s, b_s):
    bsz, seq, _ = v.shape
    n_chunks = seq // SGU_CHUNK
    v = layer_norm(v, ln_g, ln_b)
    v = v.reshape(bsz, n_chunks, SGU_CHUNK, SGU_GROUPS, SGU_GROUP_DIM)
    causal = jnp.tril(jnp.ones((SGU_CHUNK, SGU_CHUNK), dtype=bool))
    w = jnp.where(causal, w_s, 0)
    mixed = jnp.einsum('gts,bnsgc->bntgc', w, v) + b_s.T[:, :, None]
    return u * mixed.reshape(bsz, seq, D_A)


def hgrn2(q, f_pre, i, g, lb, onorm_g):
    bsz, seq, _ = q.shape
    nc = seq // HGRN_CHUNK

    def heads(t):
        return t.reshape(bsz, nc, HGRN_CHUNK, HGRN_HEADS, HGRN_HEAD_DIM).transpose(1, 0, 3, 2, 4)

    z = f_pre.astype(jnp.float32)
    lbf = lb.astype(jnp.float32)
    f = lbf + (1.0 - lbf) * jax.nn.sigmoid(z)
    log_f = jnp.log(jnp.maximum(f, F_FLOOR))
    k = (1.0 - lbf) * jax.nn.sigmoid(-z)
    qf = jax.nn.silu(q.astype(jnp.float32))
    vf = i.astype(jnp.float32)
    causal = jnp.tril(jnp.ones((HGRN_CHUNK, HGRN_CHUNK), dtype=bool))[:, :, None]

    def step(state, xs):
        qc, kc, vc, lfc = xs
        b = jnp.cumsum(lfc, axis=2)
        o_inter = jnp.einsum('bhtd,bhde->bhte', qc * jnp.exp(b), state)
        diff = b[:, :, :, None, :] - b[:, :, None, :, :]
        decay = jnp.where(causal, jnp.exp(jnp.where(causal, diff, 0.0)), 0.0)
        scores = jnp.einsum('bhtd,bhsd,bhtsd->bhts', qc, kc, decay)
        o = o_inter + jnp.einsum('bhts,bhse->bhte', scores, vc)
        b_last = b[:, :, -1:, :]
        k_dec = kc * jnp.exp(b_last - b)
        state = jnp.exp(b_last[:, :, 0, :])[..., None] * state + jnp.einsum('bhsd,bhse->bhde', k_dec, vc)
        return state, o

    s0 = jnp.zeros((bsz, HGRN_HEADS, HGRN_HEAD_DIM, HGRN_HEAD_DIM), jnp.float32)
    _, o = lax.scan(step, s0, (heads(qf), heads(k), heads(vf), heads(log_f)))
    o = o.transpose(1, 0, 3, 2, 4).reshape(bsz, seq, HGRN_HEADS, HGRN_HEAD_DIM)
    o = rms_norm(o, onorm_g.reshape(HGRN_HEADS, HGRN_HEAD_DIM))
    return (o.reshape(bsz, seq, D_B) * jax.nn.silu(g.astype(jnp.float32))).astype(q.dtype)


def even_mixer(x, w_in, sgu_g, sgu_b, w_s, b_s, lb, onorm_g, w_out):
    h = x @ w_in
    cuts = [D_A, 2 * D_A, 2 * D_A + D_B, 2 * D_A + 2 * D_B, 2 * D_A + 3 * D_B]
    a_u, a_v, b_q, b_f, b_i, b_g = jnp.split(h, cuts, axis=-1)
    a_out = spatial_gating(jax.nn.gelu(a_u), jax.nn.gelu(a_v), sgu_g, sgu_b, w_s, b_s)
    b_out = hgrn2(b_q, b_f, b_i, b_g, lb, onorm_g)
    return jnp.concatenate([a_out, b_out], axis=-1) @ w_out


def odd_mixer(x, w_in, conv_w, w_out):
    gate_b, gate_c, xin = jnp.split(x @ w_in, 3, axis=-1)
    z = gate_c * xin
    y = lax.conv_general_dilated(z, conv_w[:, None, :], window_strides=(1,),
                                 padding=[(CONV_WIDTH - 1, 0)],
                                 dimension_numbers=('NWC', 'WIO', 'NWC'),
                                 feature_group_count=D_MODEL)
    return (gate_b * y) @ w_out


def memory_attention(x, mem, w_q, w_kv, w_o):
    bsz, seq, _ = x.shape
    m_len = mem.shape[1]
    q = (x @ w_q).reshape(bsz, seq, XA_HEADS, XA_HEAD_DIM)
    k, v = jnp.split(mem @ w_kv, 2, axis=-1)
    k = k.reshape(bsz, m_len, XA_HEADS, XA_HEAD_DIM)
    v = v.reshape(bsz, m_len, XA_HEADS, XA_HEAD_DIM)
    s = jnp.einsum('bshd,bmhd->bhsm', q, k).astype(jnp.float32) * (XA_HEAD_DIM ** -0.5)
    p = jax.nn.softmax(s, axis=-1).astype(v.dtype)
    o = jnp.einsum('bhsm,bmhd->bshd', p, v).reshape(bsz, seq, D_MODEL)
    return o @ w_o


def peer_ffn(x, w_q, sub_keys, u_tab, v_tab):
    bsz, seq, d = x.shape
    t = x.reshape(-1, d)
    q = (t @ w_q).reshape(-1, PEER_HEADS, 2, PEER_HALF)
    s = jnp.einsum('thpc,pkc->thpk', q, sub_keys).astype(jnp.float32)
    top_s, top_i = lax.top_k(s, PEER_TOPK)
    cand_s = (top_s[:, :, 0, :, None] + top_s[:, :, 1, None, :]).reshape(-1, PEER_HEADS, PEER_TOPK * PEER_TOPK)
    cand_i = (top_i[:, :, 0, :, None] * PEER_NKEYS + top_i[:, :, 1, None, :]).reshape(-1, PEER_HEADS, PEER_TOPK * PEER_TOPK)
    best_s, pos = lax.top_k(cand_s, PEER_TOPK)
    expert = jnp.take_along_axis(cand_i, pos, axis=-1)
    gate = jax.nn.softmax(best_s, axis=-1).astype(x.dtype)
    nb = t.shape[0] // PEER_TOKEN_BLOCK

    def block(args):
        tb, eb, gb = args
        act = jax.nn.gelu(jnp.einsum('td,thkd->thk', tb, u_tab[eb])) * gb
        return jnp.einsum('thk,thkd->td', act, v_tab[eb])

    out = lax.map(block, (t.reshape(nb, PEER_TOKEN_BLOCK, d),
                          expert.reshape(nb, PEER_TOKEN_BLOCK, PEER_HEADS, PEER_TOPK),
                          gate.reshape(nb, PEER_TOKEN_BLOCK, PEER_HEADS, PEER_TOPK)))
    return out.reshape(bsz, seq, d)


def setup_inputs(seed: int = 0) -> dict:
    key = jax.random.key(seed)
    ks = iter(jax.random.split(key, 32))

    def nrm(shape, scale):
        return jax.random.normal(next(ks), shape, jnp.float32) * scale

    def gain(shape):
        return 1.0 + nrm(shape, 0.02)

    d = D_MODEL
    xa_k = nrm((DEPTH, d, d), d ** -0.5)
    xa_v = nrm((DEPTH, d, d), d ** -0.5 * DN_BETA)
    return {
        "x": nrm((BATCH, SEQ, d), 1.0),
        "mem": nrm((BATCH, MEM_LEN, d), 1.0),
        "ev_w_in": nrm((N_EVEN, d, EVEN_IN), d ** -0.5),
        "ev_sgu_ln_g": gain((N_EVEN, D_A)),
        "ev_sgu_ln_b": nrm((N_EVEN, D_A), 0.02),
        "ev_w_s": nrm((N_EVEN, SGU_GROUPS, SGU_CHUNK, SGU_CHUNK), SGU_CHUNK ** -0.5),
        "ev_b_s": gain((N_EVEN, SGU_GROUPS, SGU_CHUNK)),
        "ev_lb_logits": nrm((N_EVEN, D_B), 0.5),
        "ev_onorm_g": gain((N_EVEN, D_B)),
        "ev_w_out": nrm((N_EVEN, d, d), d ** -0.5 * DN_BETA),
        "od_w_in": nrm((N_ODD, d, 3 * d), d ** -0.5),
        "od_conv_w": nrm((N_ODD, CONV_WIDTH, d), CONV_WIDTH ** -0.5),
        "od_w_out": nrm((N_ODD, d, d), d ** -0.5 * DN_BETA),
        "mix_ln_g": gain((DEPTH, d)),
        "mix_ln_b": nrm((DEPTH, d), 0.02),
        "xa_w_q": nrm((DEPTH, d, d), d ** -0.5),
        "xa_w_kv": jnp.concatenate([xa_k, xa_v], axis=-1),
        "xa_w_o": nrm((DEPTH, d, d), d ** -0.5 * DN_BETA),
        "xa_ln_g": gain((DEPTH, d)),
        "xa_ln_b": nrm((DEPTH, d), 0.02),
        "peer_w_q": nrm((DEPTH, d, PEER_HEADS * PEER_QDIM), d ** -0.5),
        "peer_keys": nrm((DEPTH, 2, PEER_NKEYS, PEER_HALF), PEER_HALF ** -0.5),
        "peer_u": nrm((DEPTH, PEER_EXPERTS, d), d ** -0.5),
        "peer_v": nrm((DEPTH, PEER_EXPERTS, d), DN_BETA * PEER_HEADS ** -0.5),
        "ffn_ln_g": gain((DEPTH, d)),
        "ffn_ln_b": nrm((DEPTH, d), 0.02),
    }


def reference(x, mem, ev_w_in, ev_sgu_ln_g, ev_sgu_ln_b, ev_w_s, ev_b_s, ev_lb_logits,
              ev_onorm_g, ev_w_out, od_w_in, od_conv_w, od_w_out, mix_ln_g, mix_ln_b,
              xa_w_q, xa_w_kv, xa_w_o, xa_ln_g, xa_ln_b, peer_w_q, peer_keys, peer_u,
              peer_v, ffn_ln_g, ffn_ln_b):
    p = jax.nn.softmax(ev_lb_logits.astype(jnp.float32), axis=0)
    lower_bounds = jnp.clip(jnp.cumsum(p, axis=0) - p[0], 0.0, 1.0)
    h = x
    for layer in range(DEPTH):
        j = layer // 2
        if layer % 2 == 0:
            mix = even_mixer(h, ev_w_in[j], ev_sgu_ln_g[j], ev_sgu_ln_b[j], ev_w_s[j], ev_b_s[j],
                             lower_bounds[j], ev_onorm_g[j], ev_w_out[j])
        else:
            mix = odd_mixer(h, od_w_in[j], od_conv_w[j], od_w_out[j])
        h = layer_norm(DN_ALPHA * h + mix, mix_ln_g[layer], mix_ln_b[layer])
        h = layer_norm(DN_ALPHA * h + memory_attention(h, mem, xa_w_q[layer], xa_w_kv[layer], xa_w_o[layer]),
                       xa_ln_g[layer], xa_ln_b[layer])
        h = layer_norm(DN_ALPHA * h + peer_ffn(h, peer_w_q[layer], peer_keys[layer], peer_u[layer], peer_v[layer]),
                       ffn_ln_g[layer], ffn_ln_b[layer])
    return h
```

```python
from concourse.bass_utils import run_bass_kernel_spmd
import numpy as np
from contextlib import ExitStack
import concourse.bass as bass
import concourse.mybir as mybir

F32 = mybir.dt.float32
BF16 = mybir.dt.bfloat16
AF = mybir.ActivationFunctionType
ALU = mybir.AluOpType
AX = mybir.AxisListType

SAME_ENGINE_SYNC = True


CUR = [None]


class Buf:
    __slots__ = ("w", "r")

    def __init__(self):
        self.w = None
        self.r = {}


class Prog:
    ENGS = ("pe", "dve", "act", "pool", "sp")
    NDMA = 8

    def __init__(self, nc):
        self.nc = nc
        self.stack = ExitStack()
        self.sem = {}
        for e in ("pe", "dve", "act", "pool"):
            self.sem[e] = self.stack.enter_context(nc.semaphore("s_" + e))
        for q in ("sp", "pool", "act"):
            for i in range(self.NDMA):
                k = "d_%s%d" % (q, i)
                self.sem[k] = self.stack.enter_context(nc.semaphore(k))
        self.NCC = 0
        for i in range(self.NCC):
            k = "c_%d" % i
            self.sem[k] = self.stack.enter_context(nc.semaphore(k))
        self.cc_next = 0
        self.bg = set()
        self.gbufs = {}
        self.cnt = {k: 0 for k in self.sem}
        self.q = {e: [] for e in self.ENGS}
        self.known = {e: {} for e in self.ENGS}
        self.dma_rr = {"sp": 0, "pool": 0, "act": 0}
        self.bufs = {}
        self.nops = 0
        self.nstage = 0
        self.U = "s0_"
        CUR[0] = self

    def buf(self, key):
        b = self.bufs.get(key)
        if b is None:
            b = self.bufs[key] = Buf()
        return b

    def gbuf(self, key):
        b = self.gbufs.get(key)
        if b is None:
            b = self.gbufs[key] = Buf()
        return b

    def op(self, eng, fn, reads=(), writes=(), dma=False, cc=False, background=False):
        deps = {}

        def add(ev):
            if ev is None:
                return
            k, v = ev
            if deps.get(k, 0) < v:
                deps[k] = v

        reads = [self.buf(b) if not isinstance(b, Buf) else b for b in reads]
        writes = [self.buf(b) if not isinstance(b, Buf) else b for b in writes]
        for b in reads:
            add(b.w)
        for b in writes:
            add(b.w)
            for kv in b.r.items():
                add(kv)
        if cc:
            key = "c_%d" % self.cc_next
            self.cc_next += 1
            assert self.cc_next <= self.NCC
            self.cnt[key] = 1
            ev = (key, 1)
            inc = (key, 1)
            if background:
                self.bg.add(key)
        elif dma:
            i = self.dma_rr[eng]
            self.dma_rr[eng] = (i + 1) % self.NDMA
            key = "d_%s%d" % (eng, i)
            n = self.cnt[key]
            if n > 0:
                add((key, n))
            self.cnt[key] = n + 16
            ev = (key, n + 16)
            inc = (key, 16)
        else:
            self.cnt[eng] += 1
            ev = (eng, self.cnt[eng])
            inc = (eng, 1)
        waits = []
        kn = self.known[eng]
        for k, v in deps.items():
            if k == eng and (eng == "pe" or not SAME_ENGINE_SYNC):
                continue
            if kn.get(k, 0) >= v:
                continue
            kn[k] = v
            waits.append((k, v))
        self.q[eng].append((fn, waits, inc))
        self.nops += 1
        for b in writes:
            b.w = ev
            b.r = {}
        for b in reads:
            if b not in writes:
                if b.r.get(ev[0], 0) < ev[1]:
                    b.r[ev[0]] = ev[1]
        return ev

    def flush(self):
        nc = self.nc
        final = dict(self.cnt)
        sem = self.sem

        def emit(engname, e):
            for fn, waits, inc in self.q[engname]:
                for k, v in waits:
                    e.wait_ge(sem[k], v)
                ins = fn(e)
                ins.then_inc(sem[inc[0]], inc[1])
            kn = self.known[engname]
            for k, v in final.items():
                if k in self.bg:
                    continue
                if v > 0 and kn.get(k, 0) < v:
                    e.wait_ge(sem[k], v)
                    kn[k] = v

        with nc.Block() as block:

            @block.sync
            def _(e):
                emit("sp", e)

            @block.tensor
            def _(e):
                emit("pe", e)

            @block.vector
            def _(e):
                emit("dve", e)

            @block.scalar
            def _(e):
                emit("act", e)

            @block.gpsimd
            def _(e):
                emit("pool", e)

        self.q = {e: [] for e in self.ENGS}
        self.bufs = {}
        self.nstage += 1
        self.U = "s%d_" % self.nstage

    def dma(self, out, in_, reads=(), writes=(), q="sp"):
        return self.op(q, lambda e: e.dma_start(out=out, in_=in_), reads, writes, dma=True)

    def mm(self, out, lhsT, rhs, start, stop, reads=(), writes=()):
        return self.op("pe", lambda e: e.matmul(out, lhsT, rhs, start=start, stop=stop), reads, writes)

    def tr(self, out, in_, ident, reads=(), writes=()):
        return self.op("pe", lambda e: e.transpose(out, in_, ident), reads, writes)

    def act(self, out, in_, func, bias=None, scale=None, accum_out=None, reads=(), writes=()):
        kw = {}
        if bias is not None:
            kw["bias"] = bias
        if scale is not None:
            kw["scale"] = scale
        if accum_out is not None:
            kw["accum_out"] = accum_out
        return self.op("act", lambda e: e.activation(out, in_, func, **kw), reads, writes)

    def tt(self, eng, out, in0, in1, op, reads=(), writes=()):
        return self.op(eng, lambda e: e.tensor_tensor(out, in0, in1, op), reads, writes)

    def ts(self, eng, out, in0, s1, s2, op0, op1=None, reads=(), writes=()):
        if op1 is None:
            return self.op(eng, lambda e: e.tensor_scalar(out, in0, s1, None, op0), reads, writes)
        return self.op(eng, lambda e: e.tensor_scalar(out, in0, s1, s2, op0, op1), reads, writes)

    def stt(self, eng, out, in0, scalar, in1, op0, op1, reads=(), writes=()):
        return self.op(eng, lambda e: e.scalar_tensor_tensor(out, in0, scalar, in1, op0, op1), reads, writes)

    def copy(self, eng, out, in_, reads=(), writes=()):
        if eng == "act":
            return self.op("act", lambda e: e.copy(out, in_), reads, writes)
        return self.op(eng, lambda e: e.tensor_copy(out, in_), reads, writes)


D = 2048
KC = 16
DN_ALPHA = (2.0 * 4) ** 0.25
PEER_KEEP_WARM = (0, 0)
DUMMY_N = 64
LN_EPS = 1e-5


class Stager:
    def __init__(self, P, st, n=3):
        self.P = P
        self.bufs = [st.enter_context(P.nc.sbuf_tensor(P.U + "stg%d" % i, [128, 2048], F32)) for i in range(n)]
        self.i = 0

    def load(self, dst, src, dkey):
        k = self.i % len(self.bufs)
        eng = "act" if self.i % 2 == 0 else "dve"
        self.i += 1
        a, b = dst.shape[1], dst.shape[2]
        view = self.bufs[k][:, 0:a * b].rearrange("p (a b) -> p a b", a=a)
        self.P.dma(view, src, writes=[("stg", k)])
        self.P.copy(eng, dst, view, reads=[("stg", k)], writes=[dkey])


def load_w_bf16(P, nc, W, w_ap, ncols, wbuf, stager=None):
    src = w_ap.rearrange("(kc p) n -> p kc n", p=128)
    for kc in range(KC):
        stager.load(W[:, kc:kc + 1, :], src[:, kc:kc + 1, :], wbuf)


def stage_prep(P, cfg, x_ap, hT_ap, ident):
    nc = P.nc
    NTOK = cfg["NTOK"]
    with ExitStack() as st:
        xin = [st.enter_context(nc.sbuf_tensor(P.U + "xin%d" % i, [128, D], F32)) for i in range(2)]
        xb = [st.enter_context(nc.sbuf_tensor(P.U + "xb%d" % i, [128, D], BF16)) for i in range(2)]
        hT4 = [st.enter_context(nc.sbuf_tensor(P.U + "hT4%d" % i, [128, KC, 512], BF16)) for i in range(2)]
        pst = [st.enter_context(nc.psum_tensor(P.U + "pst%d" % i, [128, KC, 128], BF16)) for i in range(2)]
        hT_v = hT_ap.rearrange("(kc p) t -> p kc t", p=128)
        for tt in range(NTOK // 128):
            i = tt % 2
            g4 = (tt // 4) % 2
            P.dma(xin[i][:], x_ap[tt * 128:(tt + 1) * 128, :], writes=[("xin", i)])
            P.copy("act", xb[i][:], xin[i][:], reads=[("xin", i)], writes=[("xb", i)])
            for kc in range(KC):
                P.tr(pst[i][:, kc, :], xb[i][:, kc * 128:(kc + 1) * 128], ident,
                     reads=[("xb", i)], writes=[("pst", i)])
            j = tt % 4
            P.copy("dve", hT4[g4][:, :, j * 128:(j + 1) * 128], pst[i][:], reads=[("pst", i)], writes=[("hT4", g4)])
            if j == 3:
                t0 = (tt - 3) * 128
                P.dma(hT_v[:, :, t0:t0 + 512], hT4[g4][:], reads=[("hT4", g4)])
        P.flush()


def stage_gemm_ln(P, cfg, aT_ap, w_ap, hin_ap, g_ap, b_ap, hout_ap, hT_ap, ident, alpha=DN_ALPHA, want_hT=True):
    nc = P.nc
    NTOK = cfg["NTOK"]
    with ExitStack() as st:
        sb = lambda n, s, d: st.enter_context(nc.sbuf_tensor(P.U + n, s, d))
        W = sb("W", [128, KC, D], BF16)
        aT = [sb("aT%d" % i, [128, KC, 512], BF16) for i in range(2)]
        hin = [sb("hin%d" % i, [128, D], F32) for i in range(2)]
        y = [sb("y%d" % i, [128, D], F32) for i in range(2)]
        ps = [st.enter_context(nc.psum_tensor(P.U + "ps%d" % i, [128, 512], F32)) for i in range(4)]
        ln = LNCtx(P, st, g_ap, b_ap, hout_ap, hT_ap, ident, want_hT)
        load_w_bf16(P, nc, W, w_ap, D, "W", Stager(P, st))
        for tt in range(NTOK // 128):
            i = tt % 2
            j = tt % 4
            g4 = (tt // 4) % 2
            if j == 0:
                load_aT(P, aT[g4], aT_ap, tt * 128, 512, ("aT", g4))
            P.dma(hin[i][:], hin_ap[tt * 128:(tt + 1) * 128, :], writes=[("hin", i)])
            for nb in range(4):
                for kc in range(KC):
                    P.mm(ps[nb][:], aT[g4][:, kc, j * 128:(j + 1) * 128], W[:, kc, nb * 512:(nb + 1) * 512],
                         kc == 0, kc == KC - 1, reads=[("aT", g4), "W"], writes=[("ps", nb)])
            for nb in range(4):
                sl = slice(nb * 512, (nb + 1) * 512)
                P.stt("dve", y[i][:, sl], hin[i][:, sl], float(alpha), ps[nb][:], ALU.mult, ALU.add,
                      reads=[("hin", i), ("ps", nb)], writes=[("y", i)])
            ln.run(y[i][:], ("y", i), tt)
        P.flush()


class GemmCtx:
    def __init__(self, P, st, nps=4, tag="g"):
        nc = P.nc
        self.P = P
        self.tag = tag
        self.Wb = [st.enter_context(nc.sbuf_tensor(P.U + "%sWb%d" % (tag, i), [128, KC, 512], BF16)) for i in range(2)]
        self.ps = [st.enter_context(nc.psum_tensor(P.U + "%sps%d" % (tag, i), [128, 512], F32)) for i in range(nps)]
        self.wi = 0
        self.pi = 0
        self.stager = Stager(P, st)

    def next_ps(self):
        i = self.pi % len(self.ps)
        self.pi += 1
        return self.ps[i], (self.tag + "ps", i)

    def load_w(self, w_ap, c0, nc_):
        i = self.wi % 2
        self.wi += 1
        src = w_ap.rearrange("(kc p) n -> p kc n", p=128)
        key = (self.tag + "Wb", i)
        for q in range(4):
            self.stager.load(self.Wb[i][:, 4 * q:4 * q + 4, 0:nc_], src[:, 4 * q:4 * q + 4, c0:c0 + nc_], key)
        return self.Wb[i], key

    def run(self, aT, aTkey, T, w_ap, c0, ncols, mode, epi):
        P = self.P
        for cb in range(0, ncols, 512):
            nb = min(512, ncols - cb)
            Wb, wkey = self.load_w(w_ap, c0 + cb, nb)
            if mode == "tok":
                for t0 in range(0, T, 128):
                    ps, pk = self.next_ps()
                    for kc in range(KC):
                        P.mm(ps[:, 0:nb], aT[:, kc, t0:t0 + 128], Wb[:, kc, 0:nb], kc == 0, kc == KC - 1,
                             reads=[aTkey, wkey], writes=[pk])
                    epi(ps[:, 0:nb], pk, cb, t0)
            else:
                for sub in range(0, nb, 128):
                    for t0 in range(0, T, 512):
                        nt = min(512, T - t0)
                        ps, pk = self.next_ps()
                        for kc in range(KC):
                            P.mm(ps[:, 0:nt], Wb[:, kc, sub:sub + 128], aT[:, kc, t0:t0 + nt], kc == 0, kc == KC - 1,
                                 reads=[aTkey, wkey], writes=[pk])
                        epi(ps[:, 0:nt], pk, cb + sub, t0)


def _ring(st, nc, name, n, shape, dt):
    return [st.enter_context(nc.sbuf_tensor(CUR[0].U + "%s%d" % (name, i), shape, dt)) for i in range(n)]


def load_aT(P, aT, hT_ap, t0, T, key):
    v = hT_ap.rearrange("(kc p) t -> p kc t", p=128)
    for i in range(4):
        P.dma(aT[:, i * 4:(i + 1) * 4, 0:T], v[:, i * 4:(i + 1) * 4, t0:t0 + T], writes=[key])


def stage_odd_in(P, cfg, hT_ap, w_ap, cw_ap, mixT_ap):
    nc = P.nc
    S, NSEQ = cfg["S"], cfg["NSEQ"]
    NQ = (S + 511) // 512
    with ExitStack() as st:
        sb = lambda n, s, d: st.enter_context(nc.sbuf_tensor(P.U + n, s, d))
        aT = sb("aT", [128, KC, S], BF16)
        Wb = _ring(st, nc, "Wb", 2, [128, KC, 384], BF16)
        cw = sb("cw", [128, 16, 3], F32)
        gbt = _ring(st, nc, "gbt", 2, [128, S], F32)
        z = _ring(st, nc, "z", 2, [128, S + 2], F32)
        csb = _ring(st, nc, "csb", 2, [128, 512], F32)
        yb = sb("yb", [128, S], F32)
        ob = _ring(st, nc, "ob", 2, [128, S], BF16)
        ps = [st.enter_context(nc.psum_tensor(P.U + "ps%d" % i, [128, 512], F32)) for i in range(6)]
        P.dma(cw[:], cw_ap.rearrange("(fb p) k -> p fb k", p=128), writes=["cw"])
        for i in range(2):
            P.op("pool", lambda e, i=i: e.memset(z[i][:, 0:2], 0.0), writes=[("z", i)])
        wsrc = w_ap.rearrange("(kc p) n -> p kc n", p=128)
        stager = Stager(P, st)
        it = 0
        for s in range(NSEQ):
            load_aT(P, aT, hT_ap, s * S, S, "aT")
            for fb in range(16):
                fi = it % 2
                it += 1
                wk = ("Wb", fi)
                for q in range(4):
                    stager.load(Wb[fi][:, 4 * q:4 * q + 4, :], wsrc[:, 4 * q:4 * q + 4, fb * 384:(fb + 1) * 384], wk)
                for tq in range(NQ):
                    t0 = tq * 512
                    nt = min(512, S - t0)
                    pi = (tq % 2) * 3
                    for j in range(3):
                        for kc in range(KC):
                            P.mm(ps[pi + j][:, 0:nt], Wb[fi][:, kc, j * 128:(j + 1) * 128], aT[:, kc, t0:t0 + nt],
                                 kc == 0, kc == KC - 1, reads=["aT", wk], writes=[("ps", pi + j)])
                    ci = tq % 2
                    P.copy("act", gbt[fi][:, t0:t0 + nt], ps[pi][:, 0:nt], reads=[("ps", pi)], writes=[("gbt", fi)])
                    P.copy("act", csb[ci][:, 0:nt], ps[pi + 1][:, 0:nt], reads=[("ps", pi + 1)], writes=[("csb", ci)])
                    P.tt("dve", z[fi][:, 2 + t0:2 + t0 + nt], csb[ci][:, 0:nt], ps[pi + 2][:, 0:nt], ALU.mult,
                         reads=[("csb", ci), ("ps", pi + 2)], writes=[("z", fi)])
                P.ts("pool", yb[:], z[fi][:, 2:2 + S], cw[:, fb, 2:3], None, ALU.mult, reads=[("z", fi), "cw"], writes=["yb"])
                P.stt("dve", yb[:], z[fi][:, 1:1 + S], cw[:, fb, 1:2], yb[:], ALU.mult, ALU.add,
                      reads=[("z", fi), "cw"], writes=["yb"])
                P.stt("dve", yb[:], z[fi][:, 0:S], cw[:, fb, 0:1], yb[:], ALU.mult, ALU.add,
                      reads=[("z", fi), "cw"], writes=["yb"])
                P.tt("dve", ob[fi][:], yb[:], gbt[fi][:], ALU.mult, reads=["yb", ("gbt", fi)], writes=[("ob", fi)])
                P.dma(mixT_ap[fb * 128:(fb + 1) * 128, s * S:(s + 1) * S], ob[fi][:], reads=[("ob", fi)])
        P.flush()


def stage_kv(P, cfg, memT_ap, wkv_ap, kT_ap, vv_ap):
    nc = P.nc
    NSEQ = cfg["NSEQ"]
    M = NSEQ * 256
    with ExitStack() as st:
        mT = st.enter_context(nc.sbuf_tensor(P.U + "mT", [128, KC, M], BF16))
        ob = _ring(st, nc, "ob", 4, [128, 512], BF16)
        G = GemmCtx(P, st)
        for s in range(NSEQ):
            v = memT_ap[s].rearrange("(kc p) m -> p kc m", p=128)
            for q in range(2):
                G.stager.load(mT[:, 8 * q:8 * q + 8, s * 256:(s + 1) * 256], v[:, 8 * q:8 * q + 8, :], "mT")
        cnt = [0]

        def epi_k(ps, pk, c0, t0):
            i = cnt[0] % 4
            cnt[0] += 1
            n = ps.shape[1]
            P.copy("act", ob[i][:, 0:n], ps, reads=[pk], writes=[("ob", i)])
            P.dma(kT_ap[c0:c0 + 128, t0:t0 + n], ob[i][:, 0:n], reads=[("ob", i)])

        def epi_v(ps, pk, c0, t0):
            i = cnt[0] % 4
            cnt[0] += 1
            n = ps.shape[1]
            P.copy("dve", ob[i][:, 0:n], ps, reads=[pk], writes=[("ob", i)])
            P.dma(vv_ap[t0:t0 + 128, c0:c0 + n], ob[i][:, 0:n], reads=[("ob", i)])

        G.run(mT, "mT", M, wkv_ap, 0, D, "feat", epi_k)
        G.run(mT, "mT", M, wkv_ap, D, D, "tok", epi_v)
        P.flush()


def stage_attn(P, cfg, hT_ap, wq_ap, kT_ap, vv_ap, oT_ap, ident):
    nc = P.nc
    S, NSEQ = cfg["S"], cfg["NSEQ"]
    scale = 512 ** -0.5
    with ExitStack() as st:
        sb = lambda n, s, d: st.enter_context(nc.sbuf_tensor(P.U + n, s, d))
        W = sb("W", [128, KC, D], BF16)
        kT = sb("kT", [128, KC, 256], BF16)
        vv = sb("vv", [128, 2, D], BF16)
        hTg = _ring(st, nc, "hTg", 2, [128, KC, 512], BF16)
        qT = sb("qT", [128, KC, 512], BF16)
        oT4 = _ring(st, nc, "oT4", 2, [128, KC, 512], BF16)
        p32 = _ring(st, nc, "p32", 2, [128, 256], F32)
        pb = _ring(st, nc, "pb", 2, [128, 256], BF16)
        pTs = _ring(st, nc, "pTs", 2, [128, 2, 128], BF16)
        mx = _ring(st, nc, "mx", 2, [128, 1], F32)
        sm = _ring(st, nc, "sm", 2, [128, 1], F32)
        psq = [st.enter_context(nc.psum_tensor(P.U + "psq%d" % i, [128, 512], F32)) for i in range(2)]
        pss = [st.enter_context(nc.psum_tensor(P.U + "pss%d" % i, [128, 256], F32)) for i in range(2)]
        pst = [st.enter_context(nc.psum_tensor(P.U + "pst%d" % i, [128, 2, 128], BF16)) for i in range(2)]
        pso = [st.enter_context(nc.psum_tensor(P.U + "pso%d" % i, [128, 4, 128], F32)) for i in range(2)]
        load_w_bf16(P, nc, W, wq_ap, D, "W", Stager(P, st))
        oT_v = oT_ap.rearrange("(kc p) t -> p kc t", p=128)
        kT_v = kT_ap.rearrange("(kc p) m -> p kc m", p=128)
        gi = 0
        hi = 0
        for s in range(NSEQ):
            P.dma(kT[:], kT_v[:, :, s * 256:(s + 1) * 256], writes=["kT"])
            P.dma(vv[:], vv_ap[s * 256:(s + 1) * 256, :].rearrange("(c p) d -> p c d", p=128), writes=["vv"])
            for tq in range(0, S, 512):
                nt = min(512, S - tq)
                g = gi % 2
                gi += 1
                load_aT(P, hTg[g], hT_ap, s * S + tq, nt, ("hTg", g))
                for dc in range(KC):
                    q = dc % 2
                    for kc in range(KC):
                        P.mm(psq[q][:, 0:nt], W[:, kc, dc * 128:(dc + 1) * 128], hTg[g][:, kc, 0:nt], kc == 0, kc == KC - 1,
                             reads=["W", ("hTg", g)], writes=[("psq", q)])
                    P.act(qT[:, dc, 0:nt], psq[q][:, 0:nt], AF.Copy, scale=scale, reads=[("psq", q)], writes=["qT"])
                for j in range(nt // 128):
                    for h in range(4):
                        a = hi % 2
                        hi += 1
                        for dc in range(4):
                            P.mm(pss[a][:], qT[:, h * 4 + dc, j * 128:(j + 1) * 128], kT[:, h * 4 + dc, :], dc == 0, dc == 3,
                                 reads=["qT", "kT"], writes=[("pss", a)])
                        P.op("dve", lambda e, o=mx[a][:], i_=pss[a][:]: e.tensor_reduce(out=o, in_=i_, axis=AX.X, op=ALU.max, negate=True),
                             reads=[("pss", a)], writes=[("mx", a)])
                        P.act(p32[a][:], pss[a][:], AF.Exp, bias=mx[a][:], accum_out=sm[a][:],
                              reads=[("pss", a), ("mx", a)], writes=[("p32", a), ("sm", a)])
                        P.op("dve", lambda e, o=sm[a][:]: e.reciprocal(o, o), writes=[("sm", a)])
                        P.ts("dve", pb[a][:], p32[a][:], sm[a][:], None, ALU.mult, reads=[("p32", a), ("sm", a)], writes=[("pb", a)])
                        for mc in range(2):
                            P.tr(pst[a][:, mc, :], pb[a][:, mc * 128:(mc + 1) * 128], ident, reads=[("pb", a)], writes=[("pst", a)])
                        P.copy("act", pTs[a][:], pst[a][:], reads=[("pst", a)], writes=[("pTs", a)])
                        for dc in range(4):
                            for mc in range(2):
                                P.mm(pso[a][:, dc, :], vv[:, mc, (h * 4 + dc) * 128:(h * 4 + dc + 1) * 128], pTs[a][:, mc, :],
                                     mc == 0, mc == 1, reads=["vv", ("pTs", a)], writes=[("pso", a)])
                        P.copy("dve", oT4[g][:, h * 4:(h + 1) * 4, j * 128:(j + 1) * 128], pso[a][:],
                               reads=[("pso", a)], writes=[("oT4", g)])
                P.dma(oT_v[:, :, s * S + tq:s * S + tq + nt], oT4[g][:, :, 0:nt], reads=[("oT4", g)])
        P.flush()


class LNCtx:
    def __init__(self, P, st, g_ap, b_ap, hout_ap, hT_ap, ident, want_hT=True, per_tile=False):
        nc = P.nc
        self.P = P
        sb = lambda n, s, d: st.enter_context(nc.sbuf_tensor(P.U + n, s, d))
        self.gb = sb("ln_gb", [128, D], F32)
        self.bb = sb("ln_bb", [128, D], F32)
        self.hb = sb("ln_hb", [128, D], BF16)
        self.per_tile = per_tile
        self.hT4 = sb("ln_hT4", [128, KC, 128 if per_tile else 512], BF16)
        self.stats = sb("ln_stats", [128, 4, 6], F32)
        self.mv = sb("ln_mv", [128, 2], F32)
        self.rstd = sb("ln_rstd", [128, 1], F32)
        self.nmr = sb("ln_nmr", [128, 1], F32)
        self.pst32 = st.enter_context(nc.psum_tensor(P.U + "ln_pst", [128, 512], F32))
        self.pst = self.pst32[:].bitcast(BF16).rearrange("p (a b) -> p a b", a=8)
        self.hout = hout_ap
        self.hT_v = hT_ap.rearrange("(kc p) t -> p kc t", p=128) if want_hT else None
        self.ident = ident
        P.dma(self.gb[:], g_ap.partition_broadcast(128), writes=["ln_gb"])
        P.dma(self.bb[:], b_ap.partition_broadcast(128), writes=["ln_bb"])

    def run(self, y, ykey, tt):
        P = self.P
        stats, mv, rstd, nmr = self.stats, self.mv, self.rstd, self.nmr
        for nb in range(4):
            P.op("dve", lambda e, o=stats[:, nb, :], a=y[:, nb * 512:(nb + 1) * 512]: e.bn_stats(o, a),
                 reads=[ykey], writes=["ln_stats"])
        P.op("dve", lambda e: e.bn_aggr(mv[:], stats[:].rearrange("p a b -> p (a b)")), reads=["ln_stats"], writes=["ln_mv"])
        P.ts("dve", rstd[:], mv[:, 1:2], LN_EPS, None, ALU.add, reads=["ln_mv"], writes=["ln_rstd"])
        P.act(rstd[:], rstd[:], AF.Sqrt, writes=["ln_rstd"])
        P.op("dve", lambda e: e.reciprocal(rstd[:], rstd[:]), writes=["ln_rstd"])
        P.stt("dve", nmr[:], mv[:, 0:1], -1.0, rstd[:], ALU.mult, ALU.mult, reads=["ln_mv", "ln_rstd"], writes=["ln_nmr"])
        P.act(y, y, AF.Identity, bias=nmr[:], scale=rstd[:], reads=["ln_nmr", "ln_rstd"], writes=[ykey])
        P.tt("dve", y, y, self.gb[:], ALU.mult, reads=["ln_gb"], writes=[ykey])
        P.tt("pool", y, y, self.bb[:], ALU.add, reads=["ln_bb"], writes=[ykey])
        P.dma(self.hout[tt * 128:(tt + 1) * 128, :], y, reads=[ykey])
        if self.hT_v is None:
            return
        j = 0 if self.per_tile else tt % 4
        P.copy("act", self.hb[:], y, reads=[ykey], writes=["ln_hb"])
        for half in range(2):
            for k8 in range(8):
                kc = half * 8 + k8
                P.tr(self.pst[:, k8, :], self.hb[:, kc * 128:(kc + 1) * 128], self.ident, reads=["ln_hb"], writes=["ln_pst"])
            P.copy("dve", self.hT4[:, half * 8:(half + 1) * 8, j * 128:(j + 1) * 128], self.pst[:], reads=["ln_pst"], writes=["ln_hT4"])
        if self.per_tile:
            P.dma(self.hT_v[:, :, tt * 128:tt * 128 + 128], self.hT4[:], reads=["ln_hT4"])
        elif j == 3:
            t0 = (tt - 3) * 128
            P.dma(self.hT_v[:, :, t0:t0 + 512], self.hT4[:], reads=["ln_hT4"])


def stage_peer_q(P, cfg, hT_ap, wq_ap, keysT_ap, s16_ap, thr_ap):
    nc = P.nc
    NTOK = cfg["NTOK"]
    with ExitStack() as st:
        sb = lambda n, s, d: st.enter_context(nc.sbuf_tensor(P.U + n, s, d))
        W = sb("W", [128, KC, D], BF16)
        keysT = sb("keysT", [128, 2, 128], BF16)
        hTg = _ring(st, nc, "hTg", 2, [128, KC, 512], BF16)
        qT = sb("qT", [128, 16, 512], BF16)
        s16 = _ring(st, nc, "s16", 2, [128, 16, 128], BF16)
        T = sb("T", [128, 16, 16], F32)
        tmp = sb("tmp", [128, 128], BF16)
        cand = sb("cand", [128, 8, 256], F32)
        tmp2 = sb("tmp2", [128, 256], F32)
        V = sb("V", [128, 8, 16], F32)
        eV = sb("eV", [128, 8, 16], F32)
        Z = sb("Z", [128, 8], F32)
        thr = _ring(st, nc, "thr", 2, [128, 16], F32)
        psq = [st.enter_context(nc.psum_tensor(P.U + "psq%d" % i, [128, 512], F32)) for i in range(2)]
        pss = [st.enter_context(nc.psum_tensor(P.U + "pss%d" % i, [128, 512], F32)) for i in range(4)]
        load_w_bf16(P, nc, W, wq_ap, D, "W", Stager(P, st))
        P.dma(keysT[:], keysT_ap, writes=["keysT"], q="pool")
        gi = 0
        for tq in range(0, NTOK, 512):
            g = gi % 2
            gi += 1
            load_aT(P, hTg[g], hT_ap, tq, 512, ("hTg", g))
            for hp in range(16):
                q = hp % 2
                for kc in range(KC):
                    P.mm(psq[q][:], W[:, kc, hp * 128:(hp + 1) * 128], hTg[g][:, kc, :], kc == 0, kc == KC - 1,
                         reads=["W", ("hTg", g)], writes=[("psq", q)])
                P.copy("act", qT[:, hp, :], psq[q][:], reads=[("psq", q)], writes=["qT"])
            for j in range(4):
                tt = tq // 128 + j
                r = tt % 2
                for hp in range(16):
                    b = hp // 4
                    P.mm(pss[b][:, (hp % 4) * 128:(hp % 4 + 1) * 128], qT[:, hp, j * 128:(j + 1) * 128], keysT[:, hp % 2, :],
                         True, True, reads=["qT", "keysT"], writes=[("pss", b)])
                for b in range(4):
                    P.copy("act", s16[r][:, b * 4:(b + 1) * 4, :], pss[b][:].rearrange("p (a k) -> p a k", a=4),
                           reads=[("pss", b)], writes=[("s16", r)])
                P.dma(s16_ap[tt * 128:(tt + 1) * 128, :], s16[r][:].rearrange("p a k -> p (a k)"), reads=[("s16", r)])
                for hp in range(16):
                    P.op("dve", lambda e, o=T[:, hp, 0:8], i_=s16[r][:, hp, :]: e.max(out=o, in_=i_), reads=[("s16", r)], writes=["T"])
                    P.op("dve", lambda e, a=T[:, hp, 0:8], i_=s16[r][:, hp, :]: e.match_replace(out=tmp[:], in_to_replace=a, in_values=i_, imm_value=-1e30),
                         reads=[("s16", r), "T"], writes=["tmp"])
                    P.op("dve", lambda e, o=T[:, hp, 8:16]: e.max(out=o, in_=tmp[:]), reads=["tmp"], writes=["T"])
                for h in range(8):
                    P.tt("dve", cand[:, h, :].rearrange("p (a b) -> p a b", a=16),
                         T[:, 2 * h, :].unsqueeze(2).to_broadcast([128, 16, 16]),
                         T[:, 2 * h + 1, :].unsqueeze(1).to_broadcast([128, 16, 16]), ALU.add, reads=["T"], writes=["cand"])
                for h in range(8):
                    P.op("dve", lambda e, o=V[:, h, 0:8], i_=cand[:, h, :]: e.max(out=o, in_=i_), reads=["cand"], writes=["V"])
                    P.op("dve", lambda e, a=V[:, h, 0:8], i_=cand[:, h, :]: e.match_replace(out=tmp2[:], in_to_replace=a, in_values=i_, imm_value=-1e30),
                         reads=["cand", "V"], writes=["tmp2"])
                    P.op("dve", lambda e, o=V[:, h, 8:16]: e.max(out=o, in_=tmp2[:]), reads=["tmp2"], writes=["V"])
                P.tt("dve", eV[:], V[:], V[:, :, 0:1].to_broadcast([128, 8, 16]), ALU.subtract, reads=["V"], writes=["eV"])
                P.act(eV[:], eV[:], AF.Exp, writes=["eV"])
                P.op("dve", lambda e: e.tensor_reduce(out=Z[:], in_=eV[:], axis=AX.X, op=ALU.add), reads=["eV"], writes=["Z"])
                P.act(Z[:], Z[:], AF.Ln, writes=["Z"])
                P.copy("dve", thr[r][:, 0:8], V[:, :, 15], reads=["V"], writes=[("thr", r)])
                P.tt("dve", thr[r][:, 8:16], Z[:], V[:, :, 0], ALU.add, reads=["Z", "V"], writes=[("thr", r)])
                P.ts("dve", thr[r][:, 8:16], thr[r][:, 8:16], -1.0, None, ALU.mult, writes=[("thr", r)])
                P.dma(thr_ap[tt * 128:(tt + 1) * 128, :], thr[r][:], reads=[("thr", r)])
        P.flush()


def stage_peer_ffn(P, cfg, hT_ap, hin_ap, s16_ap, thr_ap, uT_ap, v_ap, g_ap, b_ap, hout_ap, hTout_ap, ident,
                   want_hT=True, n_eb=32):
    nc = P.nc
    NTOK = cfg["NTOK"]
    with ExitStack() as st:
        sb = lambda n, s, d: st.enter_context(nc.sbuf_tensor(P.U + n, s, d))
        hTb = sb("hTb", [128, KC, 512], BF16)
        sbt = sb("sbt", [128, 4, D], BF16)
        thrt = sb("thrt", [128, 4, 16], F32)
        lnw = sb("lnw", [128, 4, 8], F32)
        acc = sb("acc", [128, 4, D], F32)
        stg = sb("stg", [128, 3, 1024], F32)
        ub = _ring(st, nc, "ub", 2, [128, KC, 512], BF16)
        vb = _ring(st, nc, "vb", 2, [128, 4, D], BF16)
        ge = _ring(st, nc, "ge", 2, [128, 512], BF16)
        s1d = _ring(st, nc, "s1d", 2, [128, 8, 4], F32)
        L = _ring(st, nc, "L", 2, [128, 4, 4, 128], F32)
        Wt = _ring(st, nc, "Wt", 2, [128, 8, 4, 128], BF16)
        G = _ring(st, nc, "G", 2, [128, 512], F32)
        actb = _ring(st, nc, "actb", 2, [128, 512], BF16)
        actT = _ring(st, nc, "actT", 2, [128, 4, 128], BF16)
        psS = [st.enter_context(nc.psum_tensor(P.U + "psS%d" % i, [128, 512], F32)) for i in range(2)]
        psT = st.enter_context(nc.psum_tensor(P.U + "psT", [128, 4, 128], BF16))
        pso = [st.enter_context(nc.psum_tensor(P.U + "pso%d" % i, [128, 512], F32)) for i in range(4)]
        ln = LNCtx(P, st, g_ap, b_ap, hout_ap, hTout_ap, ident, want_hT, per_tile=True)
        u_v = uT_ap.rearrange("(kc p) e -> p kc e", p=128)
        cnt = dict(si=0, li=0)
        nblk = NTOK // 512

        ND1, ND2 = PEER_KEEP_WARM

        def keep_warm(n):
            for _ in range(n):
                P.mm(ln.pst32[:], ident, hTb[:, 0, :], True, True,
                     reads=["hTb"], writes=["ln_pst"])

        def load_pieces(ws, q0, q1):
            eb = ws % n_eb
            w = ws % 2
            vsrc = v_ap[eb * 512:(eb + 1) * 512, :].rearrange("(a p) d -> p a d", p=128)
            for q in range(q0, q1):
                k = cnt["si"] % 3
                cnt["si"] += 1
                if q < 8:
                    src = u_v[:, 2 * q:2 * q + 2, eb * 512:(eb + 1) * 512]
                    dst = ub[w][:, 2 * q:2 * q + 2, :]
                    dkey = ("ub", w)
                    sview = stg[:, k, :].rearrange("p (a b) -> p a b", a=2)
                else:
                    a4, hf = (q - 8) // 2, (q - 8) % 2
                    src = vsrc[:, a4, hf * 1024:(hf + 1) * 1024]
                    dst = vb[w][:, a4, hf * 1024:(hf + 1) * 1024]
                    dkey = ("vb", w)
                    sview = stg[:, k, :]
                P.dma(sview, src, writes=[("stg", k)])
                P.copy("act", dst, sview, reads=[("stg", k)], writes=[dkey])

        def stage_a1(ws, j, r):
            eb = ws % n_eb
            w = ws % 2
            for kc in range(KC):
                P.mm(psS[r][:], hTb[:, kc, j * 128:(j + 1) * 128], ub[w][:, kc, :], kc == 0, kc == KC - 1,
                     reads=["hTb", ("ub", w)], writes=[("psS", r)])
            P.act(ge[r][:], psS[r][:], AF.Gelu_apprx_tanh, reads=[("psS", r)], writes=[("ge", r)])
            sv = sbt[:, j, :].rearrange("p (h q k) -> p h q k", h=8, q=2)
            P.tt("pool", s1d[r][:], sv[:, :, 0, eb * 4:eb * 4 + 4], thrt[:, j, 0:8].unsqueeze(2).to_broadcast([128, 8, 4]),
                 ALU.subtract, reads=["sbt", "thrt"], writes=[("s1d", r)])
            for hh in range(2):
                hs = slice(hh * 4, hh * 4 + 4)
                P.tt("pool", L[hh][:], sv[:, hs, 1, :].unsqueeze(2).to_broadcast([128, 4, 4, 128]),
                     s1d[r][:, hs, :].unsqueeze(3).to_broadcast([128, 4, 4, 128]), ALU.add,
                     reads=["sbt", ("s1d", r)], writes=[("L", hh)])
            stage_exp(j, r, 0)

        def stage_exp(j, r, hh):
            for h4 in range(4):
                h = hh * 4 + h4
                P.act(Wt[r][:, h], L[hh][:, h4], AF.Exp, bias=lnw[:, j, h:h + 1], reads=[("L", hh), "lnw"], writes=[("Wt", r)])

        def stage_a2(ws, j, r):
            for hh in range(2):
                hs = slice(hh * 4, hh * 4 + 4)
                P.stt("dve", Wt[r][:, hs], L[hh][:], 0.0, Wt[r][:, hs], ALU.is_ge, ALU.mult, reads=[("L", hh)], writes=[("Wt", r)])

        def stage_b(ws, j, r, first):
            w = ws % 2
            P.op("dve", lambda e, o=G[r][:], i_=Wt[r][:].rearrange("p h a k -> p (a k) h"): e.tensor_reduce(out=o, in_=i_, axis=AX.X, op=ALU.add),
                 reads=[("Wt", r)], writes=[("G", r)])
            P.tt("dve", actb[r][:], ge[r][:], G[r][:], ALU.mult, reads=[("ge", r), ("G", r)], writes=[("actb", r)])
            keep_warm(ND1)
            for a4 in range(4):
                P.tr(psT[:, a4, :], actb[r][:, a4 * 128:(a4 + 1) * 128], ident, reads=[("actb", r)], writes=["psT"])
            P.copy("act", actT[r][:], psT[:], reads=["psT"], writes=[("actT", r)])

        def stage_b2(ws, j, r, first):
            w = ws % 2
            keep_warm(ND2)
            for nb in range(4):
                for a4 in range(4):
                    P.mm(pso[nb][:], actT[r][:, a4, :], vb[w][:, a4, nb * 512:(nb + 1) * 512], a4 == 0, a4 == 3,
                         reads=[("actT", r), ("vb", w)], writes=[("pso", nb)])

        def stage_c(ws, j, r, first):
            for nb in range(4):
                sl = slice(nb * 512, (nb + 1) * 512)
                if first:
                    P.copy("dve", acc[:, j, sl], pso[nb][:], reads=[("pso", nb)], writes=[("acc", j)])
                else:
                    P.tt("dve", acc[:, j, sl], acc[:, j, sl], pso[nb][:], ALU.add, reads=[("pso", nb)], writes=[("acc", j)])

        load_pieces(0, 0, 16)
        ri = 0
        for bi in range(nblk):
            tb = bi * 512
            load_aT(P, hTb, hT_ap, tb, 512, "hTb")
            P.dma(sbt[:], s16_ap[tb:tb + 512, :].rearrange("(j p) n -> p j n", p=128), writes=["sbt"])
            P.dma(thrt[:], thr_ap[tb:tb + 512, :].rearrange("(j p) n -> p j n", p=128), writes=["thrt"])
            P.tt("dve", lnw[:], thrt[:, :, 0:8], thrt[:, :, 8:16], ALU.add, reads=["thrt"], writes=["lnw"])
            items = [(bi * n_eb + eb, j, eb == 0) for eb in range(n_eb) for j in range(4)]
            infl = []
            for n in range(len(items) + 2):
                cur = None
                if n < len(items):
                    ws, j, first = items[n]
                    r = ri % 2
                    ri += 1
                    cur = (ws, j, r, first)
                    stage_a1(ws, j, r)
                b_item = infl[-1] if (n >= 1 and n - 1 < len(items)) else None
                c_item = infl[-2] if (n >= 2 and len(infl) >= 2) else None
                if n - 1 >= len(items):
                    b_item = None
                    c_item = infl[-1] if infl else None
                if b_item is not None:
                    stage_b(*b_item)
                if cur is not None:
                    stage_exp(cur[1], cur[2], 1)
                if c_item is not None:
                    stage_c(*c_item)
                if b_item is not None:
                    stage_b2(*b_item)
                if cur is not None:
                    stage_a2(cur[0], cur[1], cur[2])
                    infl.append(cur)
                    ws, j = cur[0], cur[1]
                    if not (bi == nblk - 1 and ws % n_eb == n_eb - 1):
                        load_pieces(ws + 1, 4 * j, 4 * j + 4)
                elif n == len(items):
                    pass
            for j in range(4):
                tt = tb // 128 + j
                hview = stg[:, 0:2, :].rearrange("p a b -> p (a b)")
                P.dma(hview, hin_ap[tt * 128:(tt + 1) * 128, :], writes=[("stg", 0), ("stg", 1)])
                P.stt("dve", acc[:, j, :], hview, float(DN_ALPHA), acc[:, j, :], ALU.mult, ALU.add,
                      reads=[("stg", 0), ("stg", 1)], writes=[("acc", j)])
                ln.run(acc[:, j, :], ("acc", j), tt)
        P.flush()


def stage_even_in(P, cfg, hT_ap, w_ap, lbT_ap, j_even, uT_ap, gv_ap, qsT_ap, fT_ap, vi_ap, gs_ap):
    nc = P.nc
    S, NSEQ = cfg["S"], cfg["NSEQ"]
    with ExitStack() as st:
        sb = lambda n, s, d: st.enter_context(nc.sbuf_tensor(P.U + n, s, d))
        aT = sb("aT", [128, KC, S], BF16)
        ob = _ring(st, nc, "ob", 4, [128, 512], BF16)
        of = _ring(st, nc, "of", 2, [128, 512], F32)
        lb = sb("lb", [128, 8], F32)
        oml = sb("oml", [128, 8], F32)
        l01 = sb("l01", [128, 2, 8], F32)
        G = GemmCtx(P, st)
        if j_even == 0:
            P.op("pool", lambda e: e.memset(lb[:], 0.0), writes=["lb"])
        else:
            P.dma(l01[:], lbT_ap.rearrange("l p h -> p l h"), writes=["l01"])
            P.tt("dve", lb[:], l01[:, 1, :], l01[:, 0, :], ALU.subtract, reads=["l01"], writes=["lb"])
            P.act(lb[:], lb[:], AF.Sigmoid, writes=["lb"])
        P.ts("dve", oml[:], lb[:], -1.0, 1.0, ALU.mult, ALU.add, reads=["lb"], writes=["oml"])
        cnt = [0, 0]
        for s in range(NSEQ):
            load_aT(P, aT, hT_ap, s * S, S, "aT")
            tb = s * S

            def feat_epi(func, dst, f32=False):
                def epi(ps, pk, c0, t0):
                    n = ps.shape[1]
                    if f32:
                        i = cnt[1] % 2
                        cnt[1] += 1
                        hd = c0 // 128
                        P.act(of[i][:, 0:n], ps, func, reads=[pk], writes=[("of", i)])
                        P.ts("dve", of[i][:, 0:n], of[i][:, 0:n], oml[:, hd:hd + 1], lb[:, hd:hd + 1], ALU.mult, ALU.add,
                             reads=["oml", "lb"], writes=[("of", i)])
                        P.dma(dst[c0:c0 + 128, tb + t0:tb + t0 + n], of[i][:, 0:n], reads=[("of", i)])
                    else:
                        i = cnt[0] % 4
                        cnt[0] += 1
                        P.act(ob[i][:, 0:n], ps, func, reads=[pk], writes=[("ob", i)])
                        P.dma(dst[c0:c0 + 128, tb + t0:tb + t0 + n], ob[i][:, 0:n], reads=[("ob", i)])
                return epi

            def tok_epi(func, dst):
                def epi(ps, pk, c0, t0):
                    n = ps.shape[1]
                    i = cnt[0] % 4
                    cnt[0] += 1
                    P.act(ob[i][:, 0:n], ps, func, reads=[pk], writes=[("ob", i)])
                    P.dma(dst[tb + t0:tb + t0 + 128, c0:c0 + n], ob[i][:, 0:n], reads=[("ob", i)])
                return epi

            G.run(aT, "aT", S, w_ap, 0, 1024, "feat", feat_epi(AF.Gelu_apprx_tanh, uT_ap))
            G.run(aT, "aT", S, w_ap, 1024, 1024, "tok", tok_epi(AF.Gelu_apprx_tanh, gv_ap))
            G.run(aT, "aT", S, w_ap, 2048, 1024, "feat", feat_epi(AF.Silu, qsT_ap))
            G.run(aT, "aT", S, w_ap, 5120, 1024, "tok", tok_epi(AF.Silu, gs_ap))
            G.run(aT, "aT", S, w_ap, 3072, 1024, "feat", feat_epi(AF.Sigmoid, fT_ap, f32=True))
            G.run(aT, "aT", S, w_ap, 4096, 1024, "tok", tok_epi(AF.Copy, vi_ap))
        P.flush()


def stage_even_core(P, cfg, uT_ap, gv_ap, qsT_ap, fT_ap, vi_ap, gs_ap, lng_ap, lnb_ap, wsT_ap, bs_ap, ong_ap,
                    mixT_ap, ident, tri128_ap, tri64_ap):
    nc = P.nc
    S, NSEQ = cfg["S"], cfg["NSEQ"]
    NT = S // 128
    with ExitStack() as st:
        sb = lambda n, s, d: st.enter_context(nc.sbuf_tensor(P.U + n, s, d))
        lng = sb("lng", [128, 1024], F32)
        lnb = sb("lnb", [128, 1024], F32)
        ong = sb("ong", [128, 1024], F32)
        wsT = sb("wsT", [128, 8, 128], BF16)
        ws32 = sb("ws32", [128, 8, 128], F32)
        tri128 = sb("tri128", [128, 128], F32)
        tri64 = sb("tri64", [64, 64], F32)
        bsr = sb("bsr", [1, 1024], BF16)
        ones = sb("ones", [1, 128], BF16)
        cmask = sb("cmask", [128, 1024], F32)
        P.dma(lng[:], lng_ap.partition_broadcast(128), writes=["lng"])
        P.dma(lnb[:], lnb_ap.partition_broadcast(128), writes=["lnb"])
        P.dma(ong[:], ong_ap.partition_broadcast(128), writes=["ong"])
        P.dma(ws32[:], wsT_ap, writes=["ws32"])
        P.dma(tri128[:], tri128_ap, writes=["tri128"])
        P.dma(tri64[:], tri64_ap, writes=["tri64"])
        P.dma(bsr[:], bs_ap.rearrange("(o n) -> o n", o=1), writes=["bsr"], q="pool")
        P.op("pool", lambda e: e.memset(ones[:], 1.0), writes=["ones"])
        P.op("pool", lambda e: e.memset(cmask[:], 1.0), writes=["cmask"])
        cm3 = cmask[:].rearrange("p (a b) -> p a b", b=64)
        P.op("pool", lambda e: e.memset(cm3[:, :, 0:1], 0.0), writes=["cmask"])
        P.tt("dve", wsT[:], ws32[:], tri128[:].unsqueeze(1).to_broadcast([128, 8, 128]), ALU.mult,
             reads=["ws32", "tri128"], writes=["wsT"])
        gv = _ring(st, nc, "gv", 2, [128, 1024], BF16)
        uT = _ring(st, nc, "uT", 2, [128, 8, 128], BF16)
        qsT = _ring(st, nc, "qsT", 2, [128, 8, 128], BF16)
        fT = _ring(st, nc, "fT", 2, [128, 8, 128], F32)
        vi = _ring(st, nc, "vi", 2, [64, 2, 1024], BF16)
        gs = _ring(st, nc, "gs", 2, [64, 2, 1024], BF16)
        vn = sb("vn", [128, 1024], F32)
        vln = sb("vln", [128, 1024], BF16)
        stats = sb("stats", [128, 2, 6], F32)
        mv = sb("mv", [128, 2], F32)
        rstd = sb("rstd", [128, 1], F32)
        nmr = sb("nmr", [128, 1], F32)
        aoT4 = _ring(st, nc, "aoT4", 2, [128, 8, 512], BF16)
        boT4 = _ring(st, nc, "boT4", 2, [128, 8, 512], BF16)
        lf = sb("lf", [128, 1024], F32)
        bT = sb("bT", [128, 1024], F32)
        eb = sb("eb", [128, 1024], F32)
        enb = sb("enb", [128, 1024], F32)
        kT = sb("kT", [128, 1024], F32)
        qt = sb("qt", [128, 8, 128], BF16)
        kt = sb("kt", [128, 8, 128], BF16)
        kd = sb("kd", [128, 8, 128], BF16)
        kds = sb("kds", [64, 8, 128], BF16)
        scT = sb("scT", [64, 8, 64], BF16)
        state = sb("state", [128, 8, 128], F32)
        state_bf = sb("state_bf", [128, 8, 128], BF16)
        osb = sb("osb", [64, 1024], F32)
        sq = sb("sq", [64, 1024], F32)
        ss = sb("ss", [64, 8], F32)
        bo = sb("bo", [64, 1024], BF16)
        psA = st.enter_context(nc.psum_tensor(P.U + "psA", [128, 8, 128], F32))
        pskd = st.enter_context(nc.psum_tensor(P.U + "pskd", [64, 8, 128], BF16))
        pssc = st.enter_context(nc.psum_tensor(P.U + "pssc", [64, 8, 64], F32))
        pso = st.enter_context(nc.psum_tensor(P.U + "pso", [64, 8, 128], F32))
        psbT = st.enter_context(nc.psum_tensor(P.U + "psbT", [128, 8, 128], BF16))
        uT_v = uT_ap.rearrange("(g p) t -> p g t", p=128)
        qsT_v = qsT_ap.rearrange("(g p) t -> p g t", p=128)
        fT_v = fT_ap.rearrange("(g p) t -> p g t", p=128)
        mixT_v = mixT_ap.rearrange("(g p) t -> p g t", p=128)
        eb3 = eb[:].rearrange("p (a b) -> p a b", b=64)
        for s in range(NSEQ):
            P.op("pool", lambda e: e.memset(state[:], 0.0), writes=["state"])
            P.op("pool", lambda e: e.memset(state_bf[:], 0.0), writes=["state_bf"])
            for ti in range(NT):
                tt = s * NT + ti
                t0 = tt * 128
                i = tt % 2
                j = tt % 4
                g4 = (tt // 4) % 2
                P.dma(gv[i][:], gv_ap[t0:t0 + 128, :], writes=[("gv", i)])
                P.dma(uT[i][:], uT_v[:, :, t0:t0 + 128], writes=[("uT", i)])
                P.dma(qsT[i][:], qsT_v[:, :, t0:t0 + 128], writes=[("qsT", i)])
                P.dma(fT[i][:], fT_v[:, :, t0:t0 + 128], writes=[("fT", i)])
                P.dma(vi[i][:], vi_ap[t0:t0 + 128, :].rearrange("(c p) n -> p c n", p=64), writes=[("vi", i)])
                P.dma(gs[i][:], gs_ap[t0:t0 + 128, :].rearrange("(c p) n -> p c n", p=64), writes=[("gs", i)])
                for hb in range(2):
                    P.op("dve", lambda e, o=stats[:, hb, :], a=gv[i][:, hb * 512:(hb + 1) * 512]: e.bn_stats(o, a),
                         reads=[("gv", i)], writes=["stats"])
                P.op("dve", lambda e: e.bn_aggr(mv[:], stats[:].rearrange("p a b -> p (a b)")), reads=["stats"], writes=["mv"])
                P.ts("dve", rstd[:], mv[:, 1:2], LN_EPS, None, ALU.add, reads=["mv"], writes=["rstd"])
                P.act(rstd[:], rstd[:], AF.Sqrt, writes=["rstd"])
                P.op("dve", lambda e: e.reciprocal(rstd[:], rstd[:]), writes=["rstd"])
                P.stt("dve", nmr[:], mv[:, 0:1], -1.0, rstd[:], ALU.mult, ALU.mult, reads=["mv", "rstd"], writes=["nmr"])
                P.act(vn[:], gv[i][:], AF.Identity, bias=nmr[:], scale=rstd[:], reads=[("gv", i), "nmr", "rstd"], writes=["vn"])
                P.tt("dve", vn[:], vn[:], lng[:], ALU.mult, reads=["lng"], writes=["vn"])
                P.tt("pool", vln[:], vn[:], lnb[:], ALU.add, reads=["vn", "lnb"], writes=["vln"])
                for g in range(8):
                    P.mm(psA[:, g, :], vln[:, g * 128:(g + 1) * 128], wsT[:, g, :], True, False, reads=["vln", "wsT"], writes=["psA"])
                    P.mm(psA[:, g, :], ones[:, :], bsr[:, g * 128:(g + 1) * 128], False, True, reads=["ones", "bsr"], writes=["psA"])
                P.tt("dve", aoT4[g4][:, :, j * 128:(j + 1) * 128], psA[:], uT[i][:], ALU.mult,
                     reads=["psA", ("uT", i)], writes=[("aoT4", g4)])
                fT2 = fT[i][:].rearrange("p a b -> p (a b)")
                P.ts("dve", lf[:], fT2, 1e-30, None, ALU.max, reads=[("fT", i)], writes=["lf"])
                P.act(lf[:], lf[:], AF.Ln, writes=["lf"])
                P.op("dve", lambda e: e.tensor_tensor_scan(bT[:], cmask[:], lf[:], 0.0, ALU.mult, ALU.add),
                     reads=["cmask", "lf"], writes=["bT"])
                P.act(eb[:], bT[:], AF.Exp, reads=["bT"], writes=["eb"])
                P.act(enb[:], bT[:], AF.Exp, scale=-1.0, reads=["bT"], writes=["enb"])
                P.tt("dve", qt[:].rearrange("p a b -> p (a b)"), qsT[i][:].rearrange("p a b -> p (a b)"), eb[:], ALU.mult,
                     reads=[("qsT", i), "eb"], writes=["qt"])
                P.ts("pool", kT[:], fT2, -1.0, 1.0, ALU.mult, ALU.add, reads=[("fT", i)], writes=["kT"])
                P.tt("pool", kt[:].rearrange("p a b -> p (a b)"), kT[:], enb[:], ALU.mult, reads=["kT", "enb"], writes=["kt"])
                P.tt("pool", kd[:].rearrange("p a (c b) -> p (a c) b", b=64), kt[:].rearrange("p a (c b) -> p (a c) b", b=64),
                     eb3[:, :, 63:64].to_broadcast([128, 16, 64]), ALU.mult, reads=["kt", "eb"], writes=["kd"])
                for c in range(2):
                    cs = slice(c * 64, (c + 1) * 64)
                    for h in range(8):
                        P.tr(pskd[:, h, :], kd[:, h, cs], ident, reads=["kd"], writes=["pskd"])
                    P.copy("act", kds[:], pskd[:], reads=["pskd"], writes=["kds"])
                    for h in range(8):
                        P.mm(pssc[:, h, :], kt[:, h, cs], qt[:, h, cs], True, True, reads=["kt", "qt"], writes=["pssc"])
                    P.tt("dve", scT[:], pssc[:], tri64[:].unsqueeze(1).to_broadcast([64, 8, 64]), ALU.mult,
                         reads=["pssc", "tri64"], writes=["scT"])
                    for h in range(8):
                        P.mm(pso[:, h, :], scT[:, h, :], vi[i][:, c, h * 128:(h + 1) * 128], True, False,
                             reads=["scT", ("vi", i)], writes=["pso"])
                        P.mm(pso[:, h, :], qt[:, h, cs], state_bf[:, h, :], False, True, reads=["qt", "state_bf"], writes=["pso"])
                    for h in range(8):
                        P.mm(psA[:, h, :], kds[:, h, :], vi[i][:, c, h * 128:(h + 1) * 128], True, True,
                             reads=["kds", ("vi", i)], writes=["psA"])
                    ebl = eb[:].rearrange("p (a b) -> p a b", b=128)[:, :, c * 64 + 63:c * 64 + 64]
                    P.tt("pool", state[:], state[:], ebl.to_broadcast([128, 8, 128]), ALU.mult, reads=["eb"], writes=["state"])
                    P.tt("dve", state[:], state[:], psA[:], ALU.add, reads=["psA"], writes=["state"])
                    P.copy("act", state_bf[:], state[:], reads=["state"], writes=["state_bf"])
                    P.copy("act", osb[:], pso[:].rearrange("p a b -> p (a b)"), reads=["pso"], writes=["osb"])
                    P.tt("pool", sq[:], osb[:], osb[:], ALU.mult, reads=["osb"], writes=["sq"])
                    P.op("dve", lambda e: e.tensor_reduce(out=ss[:], in_=sq[:].rearrange("p (a b) -> p a b", b=128), axis=AX.X, op=ALU.add),
                         reads=["sq"], writes=["ss"])
                    P.ts("dve", ss[:], ss[:], 1.0 / 128, LN_EPS, ALU.mult, ALU.add, writes=["ss"])
                    P.act(ss[:], ss[:], AF.Sqrt, writes=["ss"])
                    P.op("dve", lambda e: e.reciprocal(ss[:], ss[:]), writes=["ss"])
                    P.tt("dve", osb[:].rearrange("p (a b) -> p a b", b=128), osb[:].rearrange("p (a b) -> p a b", b=128),
                         ss[:].unsqueeze(2).to_broadcast([64, 8, 128]), ALU.mult, reads=["ss"], writes=["osb"])
                    P.tt("pool", osb[:], osb[:], ong[0:64, :], ALU.mult, reads=["ong"], writes=["osb"])
                    P.tt("dve", bo[:], osb[:], gs[i][:, c, :], ALU.mult, reads=["osb", ("gs", i)], writes=["bo"])
                    for h in range(8):
                        P.tr(psbT[:, h, cs], bo[:, h * 128:(h + 1) * 128], ident[0:64, 0:64], reads=["bo"], writes=["psbT"])
                P.copy("act", boT4[g4][:, :, j * 128:(j + 1) * 128], psbT[:], reads=["psbT"], writes=[("boT4", g4)])
                if j == 3 or ti == NT - 1:
                    nt = (j + 1) * 128
                    tq = t0 + 128 - nt
                    P.dma(mixT_v[:, 0:8, tq:tq + nt], aoT4[g4][:, :, 0:nt], reads=[("aoT4", g4)])
                    P.dma(mixT_v[:, 8:16, tq:tq + nt], boT4[g4][:, :, 0:nt], reads=[("boT4", g4)])
        P.flush()


N_CORES = 8
NSEQ_CORE = 16 // N_CORES
BATCH, SEQ, DEPTH, MEM_LEN = 16, 2048, 4, 256


def build_program(NSEQ=NSEQ_CORE, S=SEQ, depth=DEPTH):
    NTOK = NSEQ * S
    cfg = dict(NTOK=NTOK, S=S, NSEQ=NSEQ)
    nc = bass.Bass("TRN2", target_bir_lowering=False)

    def I(name, shape, dt=F32):
        return nc.dram_tensor(name, list(shape), dt, kind="ExternalInput").ap()

    def T(name, shape, dt):
        return nc.dram_tensor(name, list(shape), dt).ap()

    x = I("x", [NTOK, D])
    memT = I("memT", [NSEQ, D, MEM_LEN])
    ev_w_in = I("ev_w_in", [2, D, 6144])
    ev_lbT = I("ev_lbT", [2, 128, 8])
    ev_lng = I("ev_sgu_ln_g", [2, 1024])
    ev_lnb = I("ev_sgu_ln_b", [2, 1024])
    ev_wsT = I("ev_w_sT", [2, 128, 8, 128])
    ev_bs = I("ev_b_s", [2, 1024])
    ev_ong = I("ev_onorm_g", [2, 1024])
    ev_w_out = I("ev_w_out", [2, D, D])
    od_w_in = I("od_w_in_p", [2, D, 6144])
    od_cw = I("od_conv_wT", [2, D, 3])
    od_w_out = I("od_w_out", [2, D, D])
    mix_g = I("mix_ln_g", [4, D])
    mix_b = I("mix_ln_b", [4, D])
    xa_wq = I("xa_w_q", [4, D, D])
    xa_wkv = I("xa_w_kv", [4, D, 2 * D])
    xa_wo = I("xa_w_o", [4, D, D])
    xa_g = I("xa_ln_g", [4, D])
    xa_b = I("xa_ln_b", [4, D])
    pe_wq = I("peer_w_q", [4, D, D])
    pe_keysT = I("peer_keysT", [4, 128, 2, 128])
    pe_uT = I("peer_uT", [4, D, 16384])
    pe_v = I("peer_v", [4, 16384, D])
    ffn_g = I("ffn_ln_g", [4, D])
    ffn_b = I("ffn_ln_b", [4, D])
    identd = I("ident", [128, 128], BF16)
    tri128 = I("tri128", [128, 128])
    tri64 = I("tri64", [64, 64])
    out = nc.dram_tensor("out", [NTOK, D], F32, kind="ExternalOutput").ap()

    h = T("h_res", [NTOK, D], F32)
    hTs = [T("hT_a", [D, NTOK], BF16), T("hT_b", [D, NTOK], BF16)]
    mixT = T("mixT", [D, NTOK], BF16)
    kT = T("kT", [D, NSEQ * MEM_LEN], BF16)
    vv = T("vv", [NSEQ * MEM_LEN, D], BF16)
    s16 = T("s16", [NTOK, D], BF16)
    thr = T("thr", [NTOK, 16], F32)
    uTs = T("uTs", [1024, NTOK], BF16)
    gv = T("gv", [NTOK, 1024], BF16)
    qsT = T("qsT", [1024, NTOK], BF16)
    fT = T("fT", [1024, NTOK], F32)
    vi = T("vi", [NTOK, 1024], BF16)
    gs = T("gs", [NTOK, 1024], BF16)

    P = Prog(nc)
    with P.stack, nc.sbuf_tensor("ident_sb", [128, 128], BF16) as ident_t:
        ident = ident_t[:]
        P.dma(ident_t[:], identd, writes=["ident"])
        P.flush()
        cur = 0
        stage_prep(P, cfg, x, hTs[cur], ident)
        hin = x
        for layer in range(depth):
            j = layer // 2
            last = layer == depth - 1
            if layer % 2 == 0:
                stage_even_in(P, cfg, hTs[cur], ev_w_in[j], ev_lbT, j, uTs, gv, qsT, fT, vi, gs)
                stage_even_core(P, cfg, uTs, gv, qsT, fT, vi, gs, ev_lng[j], ev_lnb[j], ev_wsT[j], ev_bs[j], ev_ong[j],
                                mixT, ident, tri128, tri64)
                w_out = ev_w_out[j]
            else:
                stage_odd_in(P, cfg, hTs[cur], od_w_in[j], od_cw[j], mixT)
                w_out = od_w_out[j]
            stage_gemm_ln(P, cfg, mixT, w_out, hin, mix_g[layer], mix_b[layer], h, hTs[1 - cur], ident)
            cur = 1 - cur
            hin = h
            stage_kv(P, cfg, memT, xa_wkv[layer], kT, vv)
            stage_attn(P, cfg, hTs[cur], xa_wq[layer], kT, vv, mixT, ident)
            stage_gemm_ln(P, cfg, mixT, xa_wo[layer], h, xa_g[layer], xa_b[layer], h, hTs[1 - cur], ident)
            cur = 1 - cur
            stage_peer_q(P, cfg, hTs[cur], pe_wq[layer], pe_keysT[layer], s16, thr)
            stage_peer_ffn(P, cfg, hTs[cur], h, s16, thr, pe_uT[layer], pe_v[layer], ffn_g[layer], ffn_b[layer],
                           out if last else h, hTs[1 - cur], ident, want_hT=not last)
            cur = 1 - cur
    return nc, P


def host_layout(inp, n_cores=N_CORES, nseq=NSEQ_CORE):
    import ml_dtypes
    f = lambda a: np.ascontiguousarray(np.asarray(a, dtype=np.float32))
    shared = dict(
        ev_w_in=f(inp["ev_w_in"]),
        ev_lbT=f(np.asarray(inp["ev_lb_logits"]).reshape(2, 8, 128).transpose(0, 2, 1)),
        ev_sgu_ln_g=f(inp["ev_sgu_ln_g"]), ev_sgu_ln_b=f(inp["ev_sgu_ln_b"]),
        ev_w_sT=f(np.asarray(inp["ev_w_s"]).transpose(0, 3, 1, 2)),
        ev_b_s=f(np.asarray(inp["ev_b_s"]).reshape(2, 1024)),
        ev_onorm_g=f(inp["ev_onorm_g"]), ev_w_out=f(inp["ev_w_out"]),
        od_w_in_p=f(np.asarray(inp["od_w_in"]).reshape(2, D, 3, 16, 128).transpose(0, 1, 3, 2, 4).reshape(2, D, 6144)),
        od_conv_wT=f(np.asarray(inp["od_conv_w"]).transpose(0, 2, 1)),
        od_w_out=f(inp["od_w_out"]),
        mix_ln_g=f(inp["mix_ln_g"]), mix_ln_b=f(inp["mix_ln_b"]),
        xa_w_q=f(inp["xa_w_q"]), xa_w_kv=f(inp["xa_w_kv"]), xa_w_o=f(inp["xa_w_o"]),
        xa_ln_g=f(inp["xa_ln_g"]), xa_ln_b=f(inp["xa_ln_b"]),
        peer_w_q=f(inp["peer_w_q"]),
        peer_keysT=f(np.asarray(inp["peer_keys"]).transpose(0, 3, 1, 2)),
        peer_uT=f(np.asarray(inp["peer_u"]).transpose(0, 2, 1)),
        peer_v=f(inp["peer_v"]),
        ffn_ln_g=f(inp["ffn_ln_g"]), ffn_ln_b=f(inp["ffn_ln_b"]),
        ident=np.eye(128, dtype=np.float32).astype(ml_dtypes.bfloat16),
        tri128=np.triu(np.ones((128, 128), np.float32)),
        tri64=np.triu(np.ones((64, 64), np.float32)),
    )
    x = np.asarray(inp["x"], dtype=np.float32)
    mem = np.asarray(inp["mem"], dtype=np.float32)
    maps = []
    for c in range(n_cores):
        m = dict(shared)
        m["x"] = np.ascontiguousarray(x[c * nseq:(c + 1) * nseq].reshape(nseq * x.shape[1], D))
        m["memT"] = np.ascontiguousarray(mem[c * nseq:(c + 1) * nseq].transpose(0, 2, 1))
        maps.append(m)
    return maps


def kernel(**inputs):
    nc, _ = build_program()
    maps = host_layout(inputs)
    res = run_bass_kernel_spmd(nc, maps, core_ids=list(range(N_CORES)))
    outs = [np.asarray(r["out"]).reshape(NSEQ_CORE, SEQ, D) for r in res.results]
    return np.concatenate(outs, axis=0).astype(np.float32)
```

```python
from concourse.bass_utils import run_bass_kernel_spmd
import numpy as np
from contextlib import ExitStack
import concourse.bass as bass
import concourse.mybir as mybir

F32 = mybir.dt.float32
BF16 = mybir.dt.bfloat16
AF = mybir.ActivationFunctionType
ALU = mybir.AluOpType
AX = mybir.AxisListType

SAME_ENGINE_SYNC = True


CUR = [None]


class Buf:
    __slots__ = ("w", "r")

    def __init__(self):
        self.w = None
        self.r = {}


class Prog:
    ENGS = ("pe", "dve", "act", "pool", "sp")
    NDMA = 8

    def __init__(self, nc):
        self.nc = nc
        self.stack = ExitStack()
        self.sem = {}
        for e in ("pe", "dve", "act", "pool"):
            self.sem[e] = self.stack.enter_context(nc.semaphore("s_" + e))
        for q in ("sp", "pool", "act"):
            for i in range(self.NDMA):
                k = "d_%s%d" % (q, i)
                self.sem[k] = self.stack.enter_context(nc.semaphore(k))
        self.NCC = 0
        for i in range(self.NCC):
            k = "c_%d" % i
            self.sem[k] = self.stack.enter_context(nc.semaphore(k))
        self.cc_next = 0
        self.bg = set()
        self.gbufs = {}
        self.cnt = {k: 0 for k in self.sem}
        self.q = {e: [] for e in self.ENGS}
        self.known = {e: {} for e in self.ENGS}
        self.dma_rr = {"sp": 0, "pool": 0, "act": 0}
        self.bufs = {}
        self.nops = 0
        self.nstage = 0
        self.U = "s0_"
        CUR[0] = self

    def buf(self, key):
        b = self.bufs.get(key)
        if b is None:
            b = self.bufs[key] = Buf()
        return b

    def gbuf(self, key):
        b = self.gbufs.get(key)
        if b is None:
            b = self.gbufs[key] = Buf()
        return b

    def op(self, eng, fn, reads=(), writes=(), dma=False, cc=False, background=False):
        deps = {}

        def add(ev):
            if ev is None:
                return
            k, v = ev
            if deps.get(k, 0) < v:
                deps[k] = v

        reads = [self.buf(b) if not isinstance(b, Buf) else b for b in reads]
        writes = [self.buf(b) if not isinstance(b, Buf) else b for b in writes]
        for b in reads:
            add(b.w)
        for b in writes:
            add(b.w)
            for kv in b.r.items():
                add(kv)
        if cc:
            key = "c_%d" % self.cc_next
            self.cc_next += 1
            assert self.cc_next <= self.NCC
            self.cnt[key] = 1
            ev = (key, 1)
            inc = (key, 1)
            if background:
                self.bg.add(key)
        elif dma:
            i = self.dma_rr[eng]
            self.dma_rr[eng] = (i + 1) % self.NDMA
            key = "d_%s%d" % (eng, i)
            n = self.cnt[key]
            if n > 0:
                add((key, n))
            self.cnt[key] = n + 16
            ev = (key, n + 16)
            inc = (key, 16)
        else:
            self.cnt[eng] += 1
            ev = (eng, self.cnt[eng])
            inc = (eng, 1)
        waits = []
        kn = self.known[eng]
        for k, v in deps.items():
            if k == eng and (eng == "pe" or not SAME_ENGINE_SYNC):
                continue
            if kn.get(k, 0) >= v:
                continue
            kn[k] = v
            waits.append((k, v))
        self.q[eng].append((fn, waits, inc))
        self.nops += 1
        for b in writes:
            b.w = ev
            b.r = {}
        for b in reads:
            if b not in writes:
                if b.r.get(ev[0], 0) < ev[1]:
                    b.r[ev[0]] = ev[1]
        return ev

    def flush(self):
        nc = self.nc
        final = dict(self.cnt)
        sem = self.sem

        def emit(engname, e):
            for fn, waits, inc in self.q[engname]:
                for k, v in waits:
                    e.wait_ge(sem[k], v)
                ins = fn(e)
                ins.then_inc(sem[inc[0]], inc[1])
            kn = self.known[engname]
            for k, v in final.items():
                if k in self.bg:
                    continue
                if v > 0 and kn.get(k, 0) < v:
                    e.wait_ge(sem[k], v)
                    kn[k] = v

        with nc.Block() as block:

            @block.sync
            def _(e):
                emit("sp", e)

            @block.tensor
            def _(e):
                emit("pe", e)

            @block.vector
            def _(e):
                emit("dve", e)

            @block.scalar
            def _(e):
                emit("act", e)

            @block.gpsimd
            def _(e):
                emit("pool", e)

        self.q = {e: [] for e in self.ENGS}
        self.bufs = {}
        self.nstage += 1
        self.U = "s%d_" % self.nstage

    def dma(self, out, in_, reads=(), writes=(), q="sp"):
        return self.op(q, lambda e: e.dma_start(out=out, in_=in_), reads, writes, dma=True)

    def mm(self, out, lhsT, rhs, start, stop, reads=(), writes=()):
        return self.op("pe", lambda e: e.matmul(out, lhsT, rhs, start=start, stop=stop), reads, writes)

    def tr(self, out, in_, ident, reads=(), writes=()):
        return self.op("pe", lambda e: e.transpose(out, in_, ident), reads, writes)

    def act(self, out, in_, func, bias=None, scale=None, accum_out=None, reads=(), writes=()):
        kw = {}
        if bias is not None:
            kw["bias"] = bias
        if scale is not None:
            kw["scale"] = scale
        if accum_out is not None:
            kw["accum_out"] = accum_out
        return self.op("act", lambda e: e.activation(out, in_, func, **kw), reads, writes)

    def tt(self, eng, out, in0, in1, op, reads=(), writes=()):
        return self.op(eng, lambda e: e.tensor_tensor(out, in0, in1, op), reads, writes)

    def ts(self, eng, out, in0, s1, s2, op0, op1=None, reads=(), writes=()):
        if op1 is None:
            return self.op(eng, lambda e: e.tensor_scalar(out, in0, s1, None, op0), reads, writes)
        return self.op(eng, lambda e: e.tensor_scalar(out, in0, s1, s2, op0, op1), reads, writes)

    def stt(self, eng, out, in0, scalar, in1, op0, op1, reads=(), writes=()):
        return self.op(eng, lambda e: e.scalar_tensor_tensor(out, in0, scalar, in1, op0, op1), reads, writes)

    def copy(self, eng, out, in_, reads=(), writes=()):
        if eng == "act":
            return self.op("act", lambda e: e.copy(out, in_), reads, writes)
        return self.op(eng, lambda e: e.tensor_copy(out, in_), reads, writes)


D = 2048
KC = 16
DN_ALPHA = (2.0 * 4) ** 0.25
PEER_KEEP_WARM = (0, 0)
DUMMY_N = 64
LN_EPS = 1e-5


class Stager:
    def __init__(self, P, st, n=3):
        self.P = P
        self.bufs = [st.enter_context(P.nc.sbuf_tensor(P.U + "stg%d" % i, [128, 2048], F32)) for i in range(n)]
        self.i = 0

    def load(self, dst, src, dkey):
        k = self.i % len(self.bufs)
        eng = "act" if self.i % 2 == 0 else "dve"
        self.i += 1
        a, b = dst.shape[1], dst.shape[2]
        view = self.bufs[k][:, 0:a * b].rearrange("p (a b) -> p a b", a=a)
        self.P.dma(view, src, writes=[("stg", k)])
        self.P.copy(eng, dst, view, reads=[("stg", k)], writes=[dkey])


def load_w_bf16(P, nc, W, w_ap, ncols, wbuf, stager=None):
    src = w_ap.rearrange("(kc p) n -> p kc n", p=128)
    for kc in range(KC):
        stager.load(W[:, kc:kc + 1, :], src[:, kc:kc + 1, :], wbuf)


def stage_prep(P, cfg, x_ap, hT_ap, ident):
    nc = P.nc
    NTOK = cfg["NTOK"]
    with ExitStack() as st:
        xin = [st.enter_context(nc.sbuf_tensor(P.U + "xin%d" % i, [128, D], F32)) for i in range(2)]
        xb = [st.enter_context(nc.sbuf_tensor(P.U + "xb%d" % i, [128, D], BF16)) for i in range(2)]
        hT4 = [st.enter_context(nc.sbuf_tensor(P.U + "hT4%d" % i, [128, KC, 512], BF16)) for i in range(2)]
        pst = [st.enter_context(nc.psum_tensor(P.U + "pst%d" % i, [128, KC, 128], BF16)) for i in range(2)]
        hT_v = hT_ap.rearrange("(kc p) t -> p kc t", p=128)
        for tt in range(NTOK // 128):
            i = tt % 2
            g4 = (tt // 4) % 2
            P.dma(xin[i][:], x_ap[tt * 128:(tt + 1) * 128, :], writes=[("xin", i)])
            P.copy("act", xb[i][:], xin[i][:], reads=[("xin", i)], writes=[("xb", i)])
            for kc in range(KC):
                P.tr(pst[i][:, kc, :], xb[i][:, kc * 128:(kc + 1) * 128], ident,
                     reads=[("xb", i)], writes=[("pst", i)])
            j = tt % 4
            P.copy("dve", hT4[g4][:, :, j * 128:(j + 1) * 128], pst[i][:], reads=[("pst", i)], writes=[("hT4", g4)])
            if j == 3:
                t0 = (tt - 3) * 128
                P.dma(hT_v[:, :, t0:t0 + 512], hT4[g4][:], reads=[("hT4", g4)])
        P.flush()


def stage_gemm_ln(P, cfg, aT_ap, w_ap, hin_ap, g_ap, b_ap, hout_ap, hT_ap, ident, alpha=DN_ALPHA, want_hT=True):
    nc = P.nc
    NTOK = cfg["NTOK"]
    with ExitStack() as st:
        sb = lambda n, s, d: st.enter_context(nc.sbuf_tensor(P.U + n, s, d))
        W = sb("W", [128, KC, D], BF16)
        aT = [sb("aT%d" % i, [128, KC, 512], BF16) for i in range(2)]
        hin = [sb("hin%d" % i, [128, D], F32) for i in range(2)]
        y = [sb("y%d" % i, [128, D], F32) for i in range(2)]
        ps = [st.enter_context(nc.psum_tensor(P.U + "ps%d" % i, [128, 512], F32)) for i in range(4)]
        ln = LNCtx(P, st, g_ap, b_ap, hout_ap, hT_ap, ident, want_hT)
        load_w_bf16(P, nc, W, w_ap, D, "W", Stager(P, st))
        for tt in range(NTOK // 128):
            i = tt % 2
            j = tt % 4
            g4 = (tt // 4) % 2
            if j == 0:
                load_aT(P, aT[g4], aT_ap, tt * 128, 512, ("aT", g4))
            P.dma(hin[i][:], hin_ap[tt * 128:(tt + 1) * 128, :], writes=[("hin", i)])
            for nb in range(4):
                for kc in range(KC):
                    P.mm(ps[nb][:], aT[g4][:, kc, j * 128:(j + 1) * 128], W[:, kc, nb * 512:(nb + 1) * 512],
                         kc == 0, kc == KC - 1, reads=[("aT", g4), "W"], writes=[("ps", nb)])
            for nb in range(4):
                sl = slice(nb * 512, (nb + 1) * 512)
                P.stt("dve", y[i][:, sl], hin[i][:, sl], float(alpha), ps[nb][:], ALU.mult, ALU.add,
                      reads=[("hin", i), ("ps", nb)], writes=[("y", i)])
            ln.run(y[i][:], ("y", i), tt)
        P.flush()


class GemmCtx:
    def __init__(self, P, st, nps=4, tag="g"):
        nc = P.nc
        self.P = P
        self.tag = tag
        self.Wb = [st.enter_context(nc.sbuf_tensor(P.U + "%sWb%d" % (tag, i), [128, KC, 512], BF16)) for i in range(2)]
        self.ps = [st.enter_context(nc.psum_tensor(P.U + "%sps%d" % (tag, i), [128, 512], F32)) for i in range(nps)]
        self.wi = 0
        self.pi = 0
        self.stager = Stager(P, st)

    def next_ps(self):
        i = self.pi % len(self.ps)
        self.pi += 1
        return self.ps[i], (self.tag + "ps", i)

    def begin_load(self, w_ap, c0, nc_):
        i = self.wi % 2
        self.wi += 1
        src = w_ap.rearrange("(kc p) n -> p kc n", p=128)
        key = (self.tag + "Wb", i)

        def piece(q):
            self.stager.load(self.Wb[i][:, 4 * q:4 * q + 4, 0:nc_], src[:, 4 * q:4 * q + 4, c0:c0 + nc_], key)

        return self.Wb[i], key, piece

    def run(self, aT, aTkey, T, w_ap, c0, ncols, mode, epi):
        P = self.P
        blocks = [(cb, min(512, ncols - cb)) for cb in range(0, ncols, 512)]
        cur = self.begin_load(w_ap, c0 + blocks[0][0], blocks[0][1])
        for q in range(4):
            cur[2](q)
        for bi, (cb, nb) in enumerate(blocks):
            Wb, wkey, _ = cur
            nxt = None
            if bi + 1 < len(blocks):
                nxt = self.begin_load(w_ap, c0 + blocks[bi + 1][0], blocks[bi + 1][1])
            if mode == "tok":
                units = [("tok", t0, 0) for t0 in range(0, T, 128)]
            else:
                units = [("feat", t0, sub) for sub in range(0, nb, 128) for t0 in range(0, T, 512)]
            every = max(1, len(units) // 4)
            emitted = 0
            for ui, (kind, t0, sub) in enumerate(units):
                ps, pk = self.next_ps()
                if kind == "tok":
                    for kc in range(KC):
                        P.mm(ps[:, 0:nb], aT[:, kc, t0:t0 + 128], Wb[:, kc, 0:nb], kc == 0, kc == KC - 1,
                             reads=[aTkey, wkey], writes=[pk])
                    epi(ps[:, 0:nb], pk, cb, t0)
                else:
                    nt = min(512, T - t0)
                    for kc in range(KC):
                        P.mm(ps[:, 0:nt], Wb[:, kc, sub:sub + 128], aT[:, kc, t0:t0 + nt], kc == 0, kc == KC - 1,
                             reads=[aTkey, wkey], writes=[pk])
                    epi(ps[:, 0:nt], pk, cb + sub, t0)
                if nxt is not None and emitted < 4 and (ui + 1) % every == 0:
                    nxt[2](emitted)
                    emitted += 1
            if nxt is not None:
                while emitted < 4:
                    nxt[2](emitted)
                    emitted += 1
            cur = nxt


def _ring(st, nc, name, n, shape, dt):
    return [st.enter_context(nc.sbuf_tensor(CUR[0].U + "%s%d" % (name, i), shape, dt)) for i in range(n)]


def load_aT(P, aT, hT_ap, t0, T, key):
    v = hT_ap.rearrange("(kc p) t -> p kc t", p=128)
    for i in range(4):
        P.dma(aT[:, i * 4:(i + 1) * 4, 0:T], v[:, i * 4:(i + 1) * 4, t0:t0 + T], writes=[key])


def stage_odd_in(P, cfg, hT_ap, w_ap, cw_ap, mixT_ap):
    nc = P.nc
    S, NSEQ = cfg["S"], cfg["NSEQ"]
    NQ = (S + 511) // 512
    with ExitStack() as st:
        sb = lambda n, s, d: st.enter_context(nc.sbuf_tensor(P.U + n, s, d))
        aT = sb("aT", [128, KC, S], BF16)
        Wb = _ring(st, nc, "Wb", 2, [128, KC, 384], BF16)
        cw = sb("cw", [128, 16, 3], F32)
        gbt = _ring(st, nc, "gbt", 2, [128, S], F32)
        z = _ring(st, nc, "z", 2, [128, S + 2], F32)
        csb = _ring(st, nc, "csb", 2, [128, 512], F32)
        yb = sb("yb", [128, S], F32)
        ob = _ring(st, nc, "ob", 2, [128, S], BF16)
        ps = [st.enter_context(nc.psum_tensor(P.U + "ps%d" % i, [128, 512], F32)) for i in range(6)]
        P.dma(cw[:], cw_ap.rearrange("(fb p) k -> p fb k", p=128), writes=["cw"])
        for i in range(2):
            P.op("pool", lambda e, i=i: e.memset(z[i][:, 0:2], 0.0), writes=[("z", i)])
        wsrc = w_ap.rearrange("(kc p) n -> p kc n", p=128)
        stager = Stager(P, st)
        it = 0
        for s in range(NSEQ):
            load_aT(P, aT, hT_ap, s * S, S, "aT")
            for fb in range(16):
                fi = it % 2
                it += 1
                wk = ("Wb", fi)

                def piece(q, fb_=fb, fi_=fi, wk_=wk):
                    stager.load(Wb[fi_][:, 4 * q:4 * q + 4, :], wsrc[:, 4 * q:4 * q + 4, fb_ * 384:(fb_ + 1) * 384], wk_)

                if fb == 0 and s == 0:
                    for q in range(4):
                        piece(q)
                nfb = (fb + 1) % 16
                if not (s == NSEQ - 1 and fb == 15):
                    nfi = it % 2
                    nwk = ("Wb", nfi)
                    npiece = lambda q, fb_=nfb, fi_=nfi, wk_=nwk: stager.load(
                        Wb[fi_][:, 4 * q:4 * q + 4, :], wsrc[:, 4 * q:4 * q + 4, fb_ * 384:(fb_ + 1) * 384], wk_)
                else:
                    npiece = None
                nemit = 0
                for tq in range(NQ):
                    t0 = tq * 512
                    nt = min(512, S - t0)
                    pi = (tq % 2) * 3
                    for j in range(3):
                        for kc in range(KC):
                            P.mm(ps[pi + j][:, 0:nt], Wb[fi][:, kc, j * 128:(j + 1) * 128], aT[:, kc, t0:t0 + nt],
                                 kc == 0, kc == KC - 1, reads=["aT", wk], writes=[("ps", pi + j)])
                    ci = tq % 2
                    P.copy("act", gbt[fi][:, t0:t0 + nt], ps[pi][:, 0:nt], reads=[("ps", pi)], writes=[("gbt", fi)])
                    P.copy("act", csb[ci][:, 0:nt], ps[pi + 1][:, 0:nt], reads=[("ps", pi + 1)], writes=[("csb", ci)])
                    P.tt("dve", z[fi][:, 2 + t0:2 + t0 + nt], csb[ci][:, 0:nt], ps[pi + 2][:, 0:nt], ALU.mult,
                         reads=[("csb", ci), ("ps", pi + 2)], writes=[("z", fi)])
                    if npiece is not None and nemit < 4:
                        npiece(nemit)
                        nemit += 1
                while npiece is not None and nemit < 4:
                    npiece(nemit)
                    nemit += 1
                P.ts("pool", yb[:], z[fi][:, 2:2 + S], cw[:, fb, 2:3], None, ALU.mult, reads=[("z", fi), "cw"], writes=["yb"])
                P.stt("dve", yb[:], z[fi][:, 1:1 + S], cw[:, fb, 1:2], yb[:], ALU.mult, ALU.add,
                      reads=[("z", fi), "cw"], writes=["yb"])
                P.stt("dve", yb[:], z[fi][:, 0:S], cw[:, fb, 0:1], yb[:], ALU.mult, ALU.add,
                      reads=[("z", fi), "cw"], writes=["yb"])
                P.tt("dve", ob[fi][:], yb[:], gbt[fi][:], ALU.mult, reads=["yb", ("gbt", fi)], writes=[("ob", fi)])
                P.dma(mixT_ap[fb * 128:(fb + 1) * 128, s * S:(s + 1) * S], ob[fi][:], reads=[("ob", fi)])
        P.flush()


def stage_kv(P, cfg, memT_ap, wkv_ap, kT_ap, vv_ap):
    nc = P.nc
    NSEQ = cfg["NSEQ"]
    M = NSEQ * 256
    with ExitStack() as st:
        mT = st.enter_context(nc.sbuf_tensor(P.U + "mT", [128, KC, M], BF16))
        ob = _ring(st, nc, "ob", 4, [128, 512], BF16)
        G = GemmCtx(P, st)
        for s in range(NSEQ):
            v = memT_ap[s].rearrange("(kc p) m -> p kc m", p=128)
            for q in range(2):
                G.stager.load(mT[:, 8 * q:8 * q + 8, s * 256:(s + 1) * 256], v[:, 8 * q:8 * q + 8, :], "mT")
        cnt = [0]

        def epi_k(ps, pk, c0, t0):
            i = cnt[0] % 4
            cnt[0] += 1
            n = ps.shape[1]
            P.copy("act", ob[i][:, 0:n], ps, reads=[pk], writes=[("ob", i)])
            P.dma(kT_ap[c0:c0 + 128, t0:t0 + n], ob[i][:, 0:n], reads=[("ob", i)])

        def epi_v(ps, pk, c0, t0):
            i = cnt[0] % 4
            cnt[0] += 1
            n = ps.shape[1]
            P.copy("dve", ob[i][:, 0:n], ps, reads=[pk], writes=[("ob", i)])
            P.dma(vv_ap[t0:t0 + 128, c0:c0 + n], ob[i][:, 0:n], reads=[("ob", i)])

        G.run(mT, "mT", M, wkv_ap, 0, D, "feat", epi_k)
        G.run(mT, "mT", M, wkv_ap, D, D, "tok", epi_v)
        P.flush()


def stage_attn(P, cfg, hT_ap, wq_ap, kT_ap, vv_ap, oT_ap, ident):
    nc = P.nc
    S, NSEQ = cfg["S"], cfg["NSEQ"]
    scale = 512 ** -0.5
    with ExitStack() as st:
        sb = lambda n, s, d: st.enter_context(nc.sbuf_tensor(P.U + n, s, d))
        W = sb("W", [128, KC, D], BF16)
        kT = sb("kT", [128, KC, 256], BF16)
        vv = sb("vv", [128, 2, D], BF16)
        hTg = _ring(st, nc, "hTg", 2, [128, KC, 512], BF16)
        qT = sb("qT", [128, KC, 512], BF16)
        oT4 = _ring(st, nc, "oT4", 2, [128, KC, 512], BF16)
        p32 = _ring(st, nc, "p32", 2, [128, 256], F32)
        pb = _ring(st, nc, "pb", 2, [128, 256], BF16)
        pTs = _ring(st, nc, "pTs", 2, [128, 2, 128], BF16)
        mx = _ring(st, nc, "mx", 2, [128, 1], F32)
        sm = _ring(st, nc, "sm", 2, [128, 1], F32)
        psq = [st.enter_context(nc.psum_tensor(P.U + "psq%d" % i, [128, 512], F32)) for i in range(2)]
        pss = [st.enter_context(nc.psum_tensor(P.U + "pss%d" % i, [128, 256], F32)) for i in range(2)]
        pst = [st.enter_context(nc.psum_tensor(P.U + "pst%d" % i, [128, 2, 128], BF16)) for i in range(2)]
        pso = [st.enter_context(nc.psum_tensor(P.U + "pso%d" % i, [128, 4, 128], F32)) for i in range(2)]
        load_w_bf16(P, nc, W, wq_ap, D, "W", Stager(P, st))
        oT_v = oT_ap.rearrange("(kc p) t -> p kc t", p=128)
        kT_v = kT_ap.rearrange("(kc p) m -> p kc m", p=128)
        gi = 0
        hi = 0
        for s in range(NSEQ):
            P.dma(kT[:], kT_v[:, :, s * 256:(s + 1) * 256], writes=["kT"])
            P.dma(vv[:], vv_ap[s * 256:(s + 1) * 256, :].rearrange("(c p) d -> p c d", p=128), writes=["vv"])
            for tq in range(0, S, 512):
                nt = min(512, S - tq)
                g = gi % 2
                gi += 1
                load_aT(P, hTg[g], hT_ap, s * S + tq, nt, ("hTg", g))
                for dc in range(KC):
                    q = dc % 2
                    for kc in range(KC):
                        P.mm(psq[q][:, 0:nt], W[:, kc, dc * 128:(dc + 1) * 128], hTg[g][:, kc, 0:nt], kc == 0, kc == KC - 1,
                             reads=["W", ("hTg", g)], writes=[("psq", q)])
                    P.act(qT[:, dc, 0:nt], psq[q][:, 0:nt], AF.Copy, scale=scale, reads=[("psq", q)], writes=["qT"])
                for j in range(nt // 128):
                    for h in range(4):
                        a = hi % 2
                        hi += 1
                        for dc in range(4):
                            P.mm(pss[a][:], qT[:, h * 4 + dc, j * 128:(j + 1) * 128], kT[:, h * 4 + dc, :], dc == 0, dc == 3,
                                 reads=["qT", "kT"], writes=[("pss", a)])
                        P.op("dve", lambda e, o=mx[a][:], i_=pss[a][:]: e.tensor_reduce(out=o, in_=i_, axis=AX.X, op=ALU.max, negate=True),
                             reads=[("pss", a)], writes=[("mx", a)])
                        P.act(p32[a][:], pss[a][:], AF.Exp, bias=mx[a][:], accum_out=sm[a][:],
                              reads=[("pss", a), ("mx", a)], writes=[("p32", a), ("sm", a)])
                        P.op("dve", lambda e, o=sm[a][:]: e.reciprocal(o, o), writes=[("sm", a)])
                        P.ts("dve", pb[a][:], p32[a][:], sm[a][:], None, ALU.mult, reads=[("p32", a), ("sm", a)], writes=[("pb", a)])
                        for mc in range(2):
                            P.tr(pst[a][:, mc, :], pb[a][:, mc * 128:(mc + 1) * 128], ident, reads=[("pb", a)], writes=[("pst", a)])
                        P.copy("act", pTs[a][:], pst[a][:], reads=[("pst", a)], writes=[("pTs", a)])
                        for dc in range(4):
                            for mc in range(2):
                                P.mm(pso[a][:, dc, :], vv[:, mc, (h * 4 + dc) * 128:(h * 4 + dc + 1) * 128], pTs[a][:, mc, :],
                                     mc == 0, mc == 1, reads=["vv", ("pTs", a)], writes=[("pso", a)])
                        P.copy("dve", oT4[g][:, h * 4:(h + 1) * 4, j * 128:(j + 1) * 128], pso[a][:],
                               reads=[("pso", a)], writes=[("oT4", g)])
                P.dma(oT_v[:, :, s * S + tq:s * S + tq + nt], oT4[g][:, :, 0:nt], reads=[("oT4", g)])
        P.flush()


class LNCtx:
    def __init__(self, P, st, g_ap, b_ap, hout_ap, hT_ap, ident, want_hT=True, per_tile=False):
        nc = P.nc
        self.P = P
        sb = lambda n, s, d: st.enter_context(nc.sbuf_tensor(P.U + n, s, d))
        self.gb = sb("ln_gb", [128, D], F32)
        self.bb = sb("ln_bb", [128, D], F32)
        self.hb = sb("ln_hb", [128, D], BF16)
        self.per_tile = per_tile
        self.hT4 = sb("ln_hT4", [128, KC, 128 if per_tile else 512], BF16)
        self.stats = sb("ln_stats", [128, 4, 6], F32)
        self.mv = sb("ln_mv", [128, 2], F32)
        self.rstd = sb("ln_rstd", [128, 1], F32)
        self.nmr = sb("ln_nmr", [128, 1], F32)
        self.pst32 = st.enter_context(nc.psum_tensor(P.U + "ln_pst", [128, 512], F32))
        self.pst = self.pst32[:].bitcast(BF16).rearrange("p (a b) -> p a b", a=8)
        self.hout = hout_ap
        self.hT_v = hT_ap.rearrange("(kc p) t -> p kc t", p=128) if want_hT else None
        self.ident = ident
        P.dma(self.gb[:], g_ap.partition_broadcast(128), writes=["ln_gb"])
        P.dma(self.bb[:], b_ap.partition_broadcast(128), writes=["ln_bb"])

    def run(self, y, ykey, tt):
        P = self.P
        stats, mv, rstd, nmr = self.stats, self.mv, self.rstd, self.nmr
        for nb in range(4):
            P.op("dve", lambda e, o=stats[:, nb, :], a=y[:, nb * 512:(nb + 1) * 512]: e.bn_stats(o, a),
                 reads=[ykey], writes=["ln_stats"])
        P.op("dve", lambda e: e.bn_aggr(mv[:], stats[:].rearrange("p a b -> p (a b)")), reads=["ln_stats"], writes=["ln_mv"])
        P.ts("dve", rstd[:], mv[:, 1:2], LN_EPS, None, ALU.add, reads=["ln_mv"], writes=["ln_rstd"])
        P.act(rstd[:], rstd[:], AF.Sqrt, writes=["ln_rstd"])
        P.op("dve", lambda e: e.reciprocal(rstd[:], rstd[:]), writes=["ln_rstd"])
        P.stt("dve", nmr[:], mv[:, 0:1], -1.0, rstd[:], ALU.mult, ALU.mult, reads=["ln_mv", "ln_rstd"], writes=["ln_nmr"])
        P.act(y, y, AF.Identity, bias=nmr[:], scale=rstd[:], reads=["ln_nmr", "ln_rstd"], writes=[ykey])
        P.tt("dve", y, y, self.gb[:], ALU.mult, reads=["ln_gb"], writes=[ykey])
        P.tt("pool", y, y, self.bb[:], ALU.add, reads=["ln_bb"], writes=[ykey])
        P.dma(self.hout[tt * 128:(tt + 1) * 128, :], y, reads=[ykey])
        if self.hT_v is None:
            return
        j = 0 if self.per_tile else tt % 4
        P.copy("act", self.hb[:], y, reads=[ykey], writes=["ln_hb"])
        for half in range(2):
            for k8 in range(8):
                kc = half * 8 + k8
                P.tr(self.pst[:, k8, :], self.hb[:, kc * 128:(kc + 1) * 128], self.ident, reads=["ln_hb"], writes=["ln_pst"])
            P.copy("dve", self.hT4[:, half * 8:(half + 1) * 8, j * 128:(j + 1) * 128], self.pst[:], reads=["ln_pst"], writes=["ln_hT4"])
        if self.per_tile:
            P.dma(self.hT_v[:, :, tt * 128:tt * 128 + 128], self.hT4[:], reads=["ln_hT4"])
        elif j == 3:
            t0 = (tt - 3) * 128
            P.dma(self.hT_v[:, :, t0:t0 + 512], self.hT4[:], reads=["ln_hT4"])


def stage_peer_q(P, cfg, hT_ap, wq_ap, keysT_ap, s16_ap, thr_ap):
    nc = P.nc
    NTOK = cfg["NTOK"]
    with ExitStack() as st:
        sb = lambda n, s, d: st.enter_context(nc.sbuf_tensor(P.U + n, s, d))
        W = sb("W", [128, KC, D], BF16)
        keysT = sb("keysT", [128, 2, 128], BF16)
        hTg = _ring(st, nc, "hTg", 2, [128, KC, 512], BF16)
        qT = sb("qT", [128, 16, 512], BF16)
        s16 = _ring(st, nc, "s16", 2, [128, 16, 128], BF16)
        T = sb("T", [128, 16, 16], F32)
        tmp = sb("tmp", [128, 128], BF16)
        cand = sb("cand", [128, 8, 256], F32)
        tmp2 = sb("tmp2", [128, 256], F32)
        V = sb("V", [128, 8, 16], F32)
        eV = sb("eV", [128, 8, 16], F32)
        Z = sb("Z", [128, 8], F32)
        thr = _ring(st, nc, "thr", 2, [128, 16], F32)
        psq = [st.enter_context(nc.psum_tensor(P.U + "psq%d" % i, [128, 512], F32)) for i in range(2)]
        pss = [st.enter_context(nc.psum_tensor(P.U + "pss%d" % i, [128, 512], F32)) for i in range(4)]
        load_w_bf16(P, nc, W, wq_ap, D, "W", Stager(P, st))
        P.dma(keysT[:], keysT_ap, writes=["keysT"], q="pool")
        gi = 0
        for tq in range(0, NTOK, 512):
            g = gi % 2
            gi += 1
            load_aT(P, hTg[g], hT_ap, tq, 512, ("hTg", g))
            for hp in range(16):
                q = hp % 2
                for kc in range(KC):
                    P.mm(psq[q][:], W[:, kc, hp * 128:(hp + 1) * 128], hTg[g][:, kc, :], kc == 0, kc == KC - 1,
                         reads=["W", ("hTg", g)], writes=[("psq", q)])
                P.copy("act", qT[:, hp, :], psq[q][:], reads=[("psq", q)], writes=["qT"])
            for j in range(4):
                tt = tq // 128 + j
                r = tt % 2
                for hp in range(16):
                    b = hp // 4
                    P.mm(pss[b][:, (hp % 4) * 128:(hp % 4 + 1) * 128], qT[:, hp, j * 128:(j + 1) * 128], keysT[:, hp % 2, :],
                         True, True, reads=["qT", "keysT"], writes=[("pss", b)])
                for b in range(4):
                    P.copy("act", s16[r][:, b * 4:(b + 1) * 4, :], pss[b][:].rearrange("p (a k) -> p a k", a=4),
                           reads=[("pss", b)], writes=[("s16", r)])
                P.dma(s16_ap[tt * 128:(tt + 1) * 128, :], s16[r][:].rearrange("p a k -> p (a k)"), reads=[("s16", r)])
                for hp in range(16):
                    P.op("dve", lambda e, o=T[:, hp, 0:8], i_=s16[r][:, hp, :]: e.max(out=o, in_=i_), reads=[("s16", r)], writes=["T"])
                    P.op("dve", lambda e, a=T[:, hp, 0:8], i_=s16[r][:, hp, :]: e.match_replace(out=tmp[:], in_to_replace=a, in_values=i_, imm_value=-1e30),
                         reads=[("s16", r), "T"], writes=["tmp"])
                    P.op("dve", lambda e, o=T[:, hp, 8:16]: e.max(out=o, in_=tmp[:]), reads=["tmp"], writes=["T"])
                for h in range(8):
                    P.tt("dve", cand[:, h, :].rearrange("p (a b) -> p a b", a=16),
                         T[:, 2 * h, :].unsqueeze(2).to_broadcast([128, 16, 16]),
                         T[:, 2 * h + 1, :].unsqueeze(1).to_broadcast([128, 16, 16]), ALU.add, reads=["T"], writes=["cand"])
                for h in range(8):
                    P.op("dve", lambda e, o=V[:, h, 0:8], i_=cand[:, h, :]: e.max(out=o, in_=i_), reads=["cand"], writes=["V"])
                    P.op("dve", lambda e, a=V[:, h, 0:8], i_=cand[:, h, :]: e.match_replace(out=tmp2[:], in_to_replace=a, in_values=i_, imm_value=-1e30),
                         reads=["cand", "V"], writes=["tmp2"])
                    P.op("dve", lambda e, o=V[:, h, 8:16]: e.max(out=o, in_=tmp2[:]), reads=["tmp2"], writes=["V"])
                P.tt("dve", eV[:], V[:], V[:, :, 0:1].to_broadcast([128, 8, 16]), ALU.subtract, reads=["V"], writes=["eV"])
                P.act(eV[:], eV[:], AF.Exp, writes=["eV"])
                P.op("dve", lambda e: e.tensor_reduce(out=Z[:], in_=eV[:], axis=AX.X, op=ALU.add), reads=["eV"], writes=["Z"])
                P.act(Z[:], Z[:], AF.Ln, writes=["Z"])
                P.copy("dve", thr[r][:, 0:8], V[:, :, 15], reads=["V"], writes=[("thr", r)])
                P.tt("dve", thr[r][:, 8:16], Z[:], V[:, :, 0], ALU.add, reads=["Z", "V"], writes=[("thr", r)])
                P.ts("dve", thr[r][:, 8:16], thr[r][:, 8:16], -1.0, None, ALU.mult, writes=[("thr", r)])
                P.dma(thr_ap[tt * 128:(tt + 1) * 128, :], thr[r][:], reads=[("thr", r)])
        P.flush()


def stage_peer_ffn(P, cfg, hT_ap, hin_ap, s16_ap, thr_ap, uT_ap, v_ap, g_ap, b_ap, hout_ap, hTout_ap, ident,
                   want_hT=True, n_eb=32):
    nc = P.nc
    NTOK = cfg["NTOK"]
    with ExitStack() as st:
        sb = lambda n, s, d: st.enter_context(nc.sbuf_tensor(P.U + n, s, d))
        hTb = sb("hTb", [128, KC, 512], BF16)
        sbt = sb("sbt", [128, 4, D], BF16)
        thrt = sb("thrt", [128, 4, 16], F32)
        lnw = sb("lnw", [128, 4, 8], F32)
        acc = sb("acc", [128, 4, D], F32)
        stg = sb("stg", [128, 3, 1024], F32)
        ub = _ring(st, nc, "ub", 2, [128, KC, 512], BF16)
        vb = _ring(st, nc, "vb", 2, [128, 4, D], BF16)
        ge = _ring(st, nc, "ge", 2, [128, 512], BF16)
        s1d = _ring(st, nc, "s1d", 2, [128, 8, 4], F32)
        L = _ring(st, nc, "L", 2, [128, 4, 4, 128], F32)
        Wt = _ring(st, nc, "Wt", 2, [128, 8, 4, 128], BF16)
        G = _ring(st, nc, "G", 2, [128, 512], F32)
        actb = _ring(st, nc, "actb", 2, [128, 512], BF16)
        actT = _ring(st, nc, "actT", 2, [128, 4, 128], BF16)
        psS = [st.enter_context(nc.psum_tensor(P.U + "psS%d" % i, [128, 512], F32)) for i in range(2)]
        psT = st.enter_context(nc.psum_tensor(P.U + "psT", [128, 4, 128], BF16))
        pso = [st.enter_context(nc.psum_tensor(P.U + "pso%d" % i, [128, 512], F32)) for i in range(4)]
        ln = LNCtx(P, st, g_ap, b_ap, hout_ap, hTout_ap, ident, want_hT, per_tile=True)
        u_v = uT_ap.rearrange("(kc p) e -> p kc e", p=128)
        cnt = dict(si=0, li=0)
        nblk = NTOK // 512

        ND1, ND2 = PEER_KEEP_WARM

        def keep_warm(n):
            for _ in range(n):
                P.mm(ln.pst32[:], ident, hTb[:, 0, :], True, True,
                     reads=["hTb"], writes=["ln_pst"])

        def load_pieces(ws, q0, q1):
            eb = ws % n_eb
            w = ws % 2
            vsrc = v_ap[eb * 512:(eb + 1) * 512, :].rearrange("(a p) d -> p a d", p=128)
            for q in range(q0, q1):
                k = cnt["si"] % 3
                cnt["si"] += 1
                if q < 8:
                    src = u_v[:, 2 * q:2 * q + 2, eb * 512:(eb + 1) * 512]
                    dst = ub[w][:, 2 * q:2 * q + 2, :]
                    dkey = ("ub", w)
                    sview = stg[:, k, :].rearrange("p (a b) -> p a b", a=2)
                else:
                    a4, hf = (q - 8) // 2, (q - 8) % 2
                    src = vsrc[:, a4, hf * 1024:(hf + 1) * 1024]
                    dst = vb[w][:, a4, hf * 1024:(hf + 1) * 1024]
                    dkey = ("vb", w)
                    sview = stg[:, k, :]
                P.dma(sview, src, writes=[("stg", k)])
                P.copy("act", dst, sview, reads=[("stg", k)], writes=[dkey])

        def stage_a1(ws, j, r):
            eb = ws % n_eb
            w = ws % 2
            for kc in range(KC):
                P.mm(psS[r][:], hTb[:, kc, j * 128:(j + 1) * 128], ub[w][:, kc, :], kc == 0, kc == KC - 1,
                     reads=["hTb", ("ub", w)], writes=[("psS", r)])
            P.act(ge[r][:], psS[r][:], AF.Gelu_apprx_tanh, reads=[("psS", r)], writes=[("ge", r)])
            sv = sbt[:, j, :].rearrange("p (h q k) -> p h q k", h=8, q=2)
            P.tt("pool", s1d[r][:], sv[:, :, 0, eb * 4:eb * 4 + 4], thrt[:, j, 0:8].unsqueeze(2).to_broadcast([128, 8, 4]),
                 ALU.subtract, reads=["sbt", "thrt"], writes=[("s1d", r)])
            for hh in range(2):
                hs = slice(hh * 4, hh * 4 + 4)
                P.tt("pool", L[hh][:], sv[:, hs, 1, :].unsqueeze(2).to_broadcast([128, 4, 4, 128]),
                     s1d[r][:, hs, :].unsqueeze(3).to_broadcast([128, 4, 4, 128]), ALU.add,
                     reads=["sbt", ("s1d", r)], writes=[("L", hh)])
            stage_exp(j, r, 0)

        def stage_exp(j, r, hh):
            for h4 in range(4):
                h = hh * 4 + h4
                P.act(Wt[r][:, h], L[hh][:, h4], AF.Exp, bias=lnw[:, j, h:h + 1], reads=[("L", hh), "lnw"], writes=[("Wt", r)])

        def stage_a2(ws, j, r):
            for hh in range(2):
                hs = slice(hh * 4, hh * 4 + 4)
                P.stt("dve", Wt[r][:, hs], L[hh][:], 0.0, Wt[r][:, hs], ALU.is_ge, ALU.mult, reads=[("L", hh)], writes=[("Wt", r)])

        def stage_b(ws, j, r, first):
            w = ws % 2
            P.op("dve", lambda e, o=G[r][:], i_=Wt[r][:].rearrange("p h a k -> p (a k) h"): e.tensor_reduce(out=o, in_=i_, axis=AX.X, op=ALU.add),
                 reads=[("Wt", r)], writes=[("G", r)])
            P.tt("dve", actb[r][:], ge[r][:], G[r][:], ALU.mult, reads=[("ge", r), ("G", r)], writes=[("actb", r)])
            keep_warm(ND1)
            for a4 in range(4):
                P.tr(psT[:, a4, :], actb[r][:, a4 * 128:(a4 + 1) * 128], ident, reads=[("actb", r)], writes=["psT"])
            P.copy("act", actT[r][:], psT[:], reads=["psT"], writes=[("actT", r)])

        def stage_b2(ws, j, r, first):
            w = ws % 2
            keep_warm(ND2)
            for nb in range(4):
                for a4 in range(4):
                    P.mm(pso[nb][:], actT[r][:, a4, :], vb[w][:, a4, nb * 512:(nb + 1) * 512], a4 == 0, a4 == 3,
                         reads=[("actT", r), ("vb", w)], writes=[("pso", nb)])

        def stage_c(ws, j, r, first):
            for nb in range(4):
                sl = slice(nb * 512, (nb + 1) * 512)
                if first:
                    P.copy("dve", acc[:, j, sl], pso[nb][:], reads=[("pso", nb)], writes=[("acc", j)])
                else:
                    P.tt("dve", acc[:, j, sl], acc[:, j, sl], pso[nb][:], ALU.add, reads=[("pso", nb)], writes=[("acc", j)])

        load_pieces(0, 0, 16)
        ri = 0
        for bi in range(nblk):
            tb = bi * 512
            load_aT(P, hTb, hT_ap, tb, 512, "hTb")
            P.dma(sbt[:], s16_ap[tb:tb + 512, :].rearrange("(j p) n -> p j n", p=128), writes=["sbt"])
            P.dma(thrt[:], thr_ap[tb:tb + 512, :].rearrange("(j p) n -> p j n", p=128), writes=["thrt"])
            P.tt("dve", lnw[:], thrt[:, :, 0:8], thrt[:, :, 8:16], ALU.add, reads=["thrt"], writes=["lnw"])
            items = [(bi * n_eb + eb, j, eb == 0) for eb in range(n_eb) for j in range(4)]
            infl = []
            for n in range(len(items) + 2):
                cur = None
                if n < len(items):
                    ws, j, first = items[n]
                    r = ri % 2
                    ri += 1
                    cur = (ws, j, r, first)
                    stage_a1(ws, j, r)
                b_item = infl[-1] if (n >= 1 and n - 1 < len(items)) else None
                c_item = infl[-2] if (n >= 2 and len(infl) >= 2) else None
                if n - 1 >= len(items):
                    b_item = None
                    c_item = infl[-1] if infl else None
                if b_item is not None:
                    stage_b(*b_item)
                if cur is not None:
                    stage_exp(cur[1], cur[2], 1)
                if c_item is not None:
                    stage_c(*c_item)
                if b_item is not None:
                    stage_b2(*b_item)
                if cur is not None:
                    stage_a2(cur[0], cur[1], cur[2])
                    infl.append(cur)
                    ws, j = cur[0], cur[1]
                    if not (bi == nblk - 1 and ws % n_eb == n_eb - 1):
                        load_pieces(ws + 1, 4 * j, 4 * j + 4)
                elif n == len(items):
                    pass
            for j in range(4):
                tt = tb // 128 + j
                hview = stg[:, 0:2, :].rearrange("p a b -> p (a b)")
                P.dma(hview, hin_ap[tt * 128:(tt + 1) * 128, :], writes=[("stg", 0), ("stg", 1)])
                P.stt("dve", acc[:, j, :], hview, float(DN_ALPHA), acc[:, j, :], ALU.mult, ALU.add,
                      reads=[("stg", 0), ("stg", 1)], writes=[("acc", j)])
                ln.run(acc[:, j, :], ("acc", j), tt)
        P.flush()


def stage_even_in(P, cfg, hT_ap, w_ap, lbT_ap, j_even, uT_ap, gv_ap, qsT_ap, fT_ap, vi_ap, gs_ap):
    nc = P.nc
    S, NSEQ = cfg["S"], cfg["NSEQ"]
    with ExitStack() as st:
        sb = lambda n, s, d: st.enter_context(nc.sbuf_tensor(P.U + n, s, d))
        aT = sb("aT", [128, KC, S], BF16)
        ob = _ring(st, nc, "ob", 4, [128, 512], BF16)
        of = _ring(st, nc, "of", 2, [128, 512], F32)
        lb = sb("lb", [128, 8], F32)
        oml = sb("oml", [128, 8], F32)
        l01 = sb("l01", [128, 2, 8], F32)
        G = GemmCtx(P, st)
        if j_even == 0:
            P.op("pool", lambda e: e.memset(lb[:], 0.0), writes=["lb"])
        else:
            P.dma(l01[:], lbT_ap.rearrange("l p h -> p l h"), writes=["l01"])
            P.tt("dve", lb[:], l01[:, 1, :], l01[:, 0, :], ALU.subtract, reads=["l01"], writes=["lb"])
            P.act(lb[:], lb[:], AF.Sigmoid, writes=["lb"])
        P.ts("dve", oml[:], lb[:], -1.0, 1.0, ALU.mult, ALU.add, reads=["lb"], writes=["oml"])
        cnt = [0, 0]
        for s in range(NSEQ):
            load_aT(P, aT, hT_ap, s * S, S, "aT")
            tb = s * S

            def feat_epi(func, dst, f32=False):
                def epi(ps, pk, c0, t0):
                    n = ps.shape[1]
                    if f32:
                        i = cnt[1] % 2
                        cnt[1] += 1
                        hd = c0 // 128
                        P.act(of[i][:, 0:n], ps, func, reads=[pk], writes=[("of", i)])
                        P.ts("dve", of[i][:, 0:n], of[i][:, 0:n], oml[:, hd:hd + 1], lb[:, hd:hd + 1], ALU.mult, ALU.add,
                             reads=["oml", "lb"], writes=[("of", i)])
                        P.dma(dst[c0:c0 + 128, tb + t0:tb + t0 + n], of[i][:, 0:n], reads=[("of", i)])
                    else:
                        i = cnt[0] % 4
                        cnt[0] += 1
                        P.act(ob[i][:, 0:n], ps, func, reads=[pk], writes=[("ob", i)])
                        P.dma(dst[c0:c0 + 128, tb + t0:tb + t0 + n], ob[i][:, 0:n], reads=[("ob", i)])
                return epi

            def tok_epi(func, dst):
                def epi(ps, pk, c0, t0):
                    n = ps.shape[1]
                    i = cnt[0] % 4
                    cnt[0] += 1
                    P.act(ob[i][:, 0:n], ps, func, reads=[pk], writes=[("ob", i)])
                    P.dma(dst[tb + t0:tb + t0 + 128, c0:c0 + n], ob[i][:, 0:n], reads=[("ob", i)])
                return epi

            G.run(aT, "aT", S, w_ap, 0, 1024, "feat", feat_epi(AF.Gelu_apprx_tanh, uT_ap))
            G.run(aT, "aT", S, w_ap, 1024, 1024, "tok", tok_epi(AF.Gelu_apprx_tanh, gv_ap))
            G.run(aT, "aT", S, w_ap, 2048, 1024, "feat", feat_epi(AF.Silu, qsT_ap))
            G.run(aT, "aT", S, w_ap, 5120, 1024, "tok", tok_epi(AF.Silu, gs_ap))
            G.run(aT, "aT", S, w_ap, 3072, 1024, "feat", feat_epi(AF.Sigmoid, fT_ap, f32=True))
            G.run(aT, "aT", S, w_ap, 4096, 1024, "tok", tok_epi(AF.Copy, vi_ap))
        P.flush()


def stage_even_core(P, cfg, uT_ap, gv_ap, qsT_ap, fT_ap, vi_ap, gs_ap, lng_ap, lnb_ap, wsT_ap, bs_ap, ong_ap,
                    mixT_ap, ident, tri128_ap, tri64_ap):
    nc = P.nc
    S, NSEQ = cfg["S"], cfg["NSEQ"]
    NT = S // 128
    with ExitStack() as st:
        sb = lambda n, s, d: st.enter_context(nc.sbuf_tensor(P.U + n, s, d))
        lng = sb("lng", [128, 1024], F32)
        lnb = sb("lnb", [128, 1024], F32)
        ong = sb("ong", [128, 1024], F32)
        wsT = sb("wsT", [128, 8, 128], BF16)
        ws32 = sb("ws32", [128, 8, 128], F32)
        tri128 = sb("tri128", [128, 128], F32)
        tri64 = sb("tri64", [64, 64], F32)
        bsr = sb("bsr", [1, 1024], BF16)
        ones = sb("ones", [1, 128], BF16)
        cmask = sb("cmask", [128, 1024], F32)
        P.dma(lng[:], lng_ap.partition_broadcast(128), writes=["lng"])
        P.dma(lnb[:], lnb_ap.partition_broadcast(128), writes=["lnb"])
        P.dma(ong[:], ong_ap.partition_broadcast(128), writes=["ong"])
        P.dma(ws32[:], wsT_ap, writes=["ws32"])
        P.dma(tri128[:], tri128_ap, writes=["tri128"])
        P.dma(tri64[:], tri64_ap, writes=["tri64"])
        P.dma(bsr[:], bs_ap.rearrange("(o n) -> o n", o=1), writes=["bsr"], q="pool")
        P.op("pool", lambda e: e.memset(ones[:], 1.0), writes=["ones"])
        P.op("pool", lambda e: e.memset(cmask[:], 1.0), writes=["cmask"])
        cm3 = cmask[:].rearrange("p (a b) -> p a b", b=64)
        P.op("pool", lambda e: e.memset(cm3[:, :, 0:1], 0.0), writes=["cmask"])
        P.tt("dve", wsT[:], ws32[:], tri128[:].unsqueeze(1).to_broadcast([128, 8, 128]), ALU.mult,
             reads=["ws32", "tri128"], writes=["wsT"])
        gv = _ring(st, nc, "gv", 2, [128, 1024], BF16)
        uT = _ring(st, nc, "uT", 2, [128, 8, 128], BF16)
        qsT = _ring(st, nc, "qsT", 2, [128, 8, 128], BF16)
        fT = _ring(st, nc, "fT", 2, [128, 8, 128], F32)
        vi = _ring(st, nc, "vi", 2, [64, 2, 1024], BF16)
        gs = _ring(st, nc, "gs", 2, [64, 2, 1024], BF16)
        vn = sb("vn", [128, 1024], F32)
        vln = sb("vln", [128, 1024], BF16)
        stats = sb("stats", [128, 2, 6], F32)
        mv = sb("mv", [128, 2], F32)
        rstd = sb("rstd", [128, 1], F32)
        nmr = sb("nmr", [128, 1], F32)
        aoT4 = _ring(st, nc, "aoT4", 2, [128, 8, 512], BF16)
        boT4 = _ring(st, nc, "boT4", 2, [128, 8, 512], BF16)
        lf = sb("lf", [128, 1024], F32)
        bT = sb("bT", [128, 1024], F32)
        eb = sb("eb", [128, 1024], F32)
        enb = sb("enb", [128, 1024], F32)
        kT = sb("kT", [128, 1024], F32)
        qt = sb("qt", [128, 8, 128], BF16)
        kt = sb("kt", [128, 8, 128], BF16)
        kd = sb("kd", [128, 8, 128], BF16)
        kds = sb("kds", [64, 8, 128], BF16)
        scT = sb("scT", [64, 8, 64], BF16)
        state = sb("state", [128, 8, 128], F32)
        state_bf = sb("state_bf", [128, 8, 128], BF16)
        osb = sb("osb", [64, 1024], F32)
        sq = sb("sq", [64, 1024], F32)
        ss = sb("ss", [64, 8], F32)
        bo = sb("bo", [64, 1024], BF16)
        psA = st.enter_context(nc.psum_tensor(P.U + "psA", [128, 8, 128], F32))
        pskd = st.enter_context(nc.psum_tensor(P.U + "pskd", [64, 8, 128], BF16))
        pssc = st.enter_context(nc.psum_tensor(P.U + "pssc", [64, 8, 64], F32))
        pso = st.enter_context(nc.psum_tensor(P.U + "pso", [64, 8, 128], F32))
        psbT = st.enter_context(nc.psum_tensor(P.U + "psbT", [128, 8, 128], BF16))
        uT_v = uT_ap.rearrange("(g p) t -> p g t", p=128)
        qsT_v = qsT_ap.rearrange("(g p) t -> p g t", p=128)
        fT_v = fT_ap.rearrange("(g p) t -> p g t", p=128)
        mixT_v = mixT_ap.rearrange("(g p) t -> p g t", p=128)
        eb3 = eb[:].rearrange("p (a b) -> p a b", b=64)
        for s in range(NSEQ):
            P.op("pool", lambda e: e.memset(state[:], 0.0), writes=["state"])
            P.op("pool", lambda e: e.memset(state_bf[:], 0.0), writes=["state_bf"])
            for ti in range(NT):
                tt = s * NT + ti
                t0 = tt * 128
                i = tt % 2
                j = tt % 4
                g4 = (tt // 4) % 2
                P.dma(gv[i][:], gv_ap[t0:t0 + 128, :], writes=[("gv", i)])
                P.dma(uT[i][:], uT_v[:, :, t0:t0 + 128], writes=[("uT", i)])
                P.dma(qsT[i][:], qsT_v[:, :, t0:t0 + 128], writes=[("qsT", i)])
                P.dma(fT[i][:], fT_v[:, :, t0:t0 + 128], writes=[("fT", i)])
                P.dma(vi[i][:], vi_ap[t0:t0 + 128, :].rearrange("(c p) n -> p c n", p=64), writes=[("vi", i)])
                P.dma(gs[i][:], gs_ap[t0:t0 + 128, :].rearrange("(c p) n -> p c n", p=64), writes=[("gs", i)])
                for hb in range(2):
                    P.op("dve", lambda e, o=stats[:, hb, :], a=gv[i][:, hb * 512:(hb + 1) * 512]: e.bn_stats(o, a),
                         reads=[("gv", i)], writes=["stats"])
                P.op("dve", lambda e: e.bn_aggr(mv[:], stats[:].rearrange("p a b -> p (a b)")), reads=["stats"], writes=["mv"])
                P.ts("dve", rstd[:], mv[:, 1:2], LN_EPS, None, ALU.add, reads=["mv"], writes=["rstd"])
                P.act(rstd[:], rstd[:], AF.Sqrt, writes=["rstd"])
                P.op("dve", lambda e: e.reciprocal(rstd[:], rstd[:]), writes=["rstd"])
                P.stt("dve", nmr[:], mv[:, 0:1], -1.0, rstd[:], ALU.mult, ALU.mult, reads=["mv", "rstd"], writes=["nmr"])
                P.act(vn[:], gv[i][:], AF.Identity, bias=nmr[:], scale=rstd[:], reads=[("gv", i), "nmr", "rstd"], writes=["vn"])
                P.tt("dve", vn[:], vn[:], lng[:], ALU.mult, reads=["lng"], writes=["vn"])
                P.tt("pool", vln[:], vn[:], lnb[:], ALU.add, reads=["vn", "lnb"], writes=["vln"])
                for g in range(8):
                    P.mm(psA[:, g, :], vln[:, g * 128:(g + 1) * 128], wsT[:, g, :], True, False, reads=["vln", "wsT"], writes=["psA"])
                    P.mm(psA[:, g, :], ones[:, :], bsr[:, g * 128:(g + 1) * 128], False, True, reads=["ones", "bsr"], writes=["psA"])
                P.tt("dve", aoT4[g4][:, :, j * 128:(j + 1) * 128], psA[:], uT[i][:], ALU.mult,
                     reads=["psA", ("uT", i)], writes=[("aoT4", g4)])
                fT2 = fT[i][:].rearrange("p a b -> p (a b)")
                P.ts("dve", lf[:], fT2, 1e-30, None, ALU.max, reads=[("fT", i)], writes=["lf"])
                P.act(lf[:], lf[:], AF.Ln, writes=["lf"])
                P.op("dve", lambda e: e.tensor_tensor_scan(bT[:], cmask[:], lf[:], 0.0, ALU.mult, ALU.add),
                     reads=["cmask", "lf"], writes=["bT"])
                P.act(eb[:], bT[:], AF.Exp, reads=["bT"], writes=["eb"])
                P.act(enb[:], bT[:], AF.Exp, scale=-1.0, reads=["bT"], writes=["enb"])
                P.tt("dve", qt[:].rearrange("p a b -> p (a b)"), qsT[i][:].rearrange("p a b -> p (a b)"), eb[:], ALU.mult,
                     reads=[("qsT", i), "eb"], writes=["qt"])
                P.ts("pool", kT[:], fT2, -1.0, 1.0, ALU.mult, ALU.add, reads=[("fT", i)], writes=["kT"])
                P.tt("pool", kt[:].rearrange("p a b -> p (a b)"), kT[:], enb[:], ALU.mult, reads=["kT", "enb"], writes=["kt"])
                P.tt("pool", kd[:].rearrange("p a (c b) -> p (a c) b", b=64), kt[:].rearrange("p a (c b) -> p (a c) b", b=64),
                     eb3[:, :, 63:64].to_broadcast([128, 16, 64]), ALU.mult, reads=["kt", "eb"], writes=["kd"])
                for c in range(2):
                    cs = slice(c * 64, (c + 1) * 64)
                    for h in range(8):
                        P.tr(pskd[:, h, :], kd[:, h, cs], ident, reads=["kd"], writes=["pskd"])
                    P.copy("act", kds[:], pskd[:], reads=["pskd"], writes=["kds"])
                    for h in range(8):
                        P.mm(pssc[:, h, :], kt[:, h, cs], qt[:, h, cs], True, True, reads=["kt", "qt"], writes=["pssc"])
                    P.tt("dve", scT[:], pssc[:], tri64[:].unsqueeze(1).to_broadcast([64, 8, 64]), ALU.mult,
                         reads=["pssc", "tri64"], writes=["scT"])
                    for h in range(8):
                        P.mm(pso[:, h, :], scT[:, h, :], vi[i][:, c, h * 128:(h + 1) * 128], True, False,
                             reads=["scT", ("vi", i)], writes=["pso"])
                        P.mm(pso[:, h, :], qt[:, h, cs], state_bf[:, h, :], False, True, reads=["qt", "state_bf"], writes=["pso"])
                    for h in range(8):
                        P.mm(psA[:, h, :], kds[:, h, :], vi[i][:, c, h * 128:(h + 1) * 128], True, True,
                             reads=["kds", ("vi", i)], writes=["psA"])
                    ebl = eb[:].rearrange("p (a b) -> p a b", b=128)[:, :, c * 64 + 63:c * 64 + 64]
                    P.tt("pool", state[:], state[:], ebl.to_broadcast([128, 8, 128]), ALU.mult, reads=["eb"], writes=["state"])
                    P.tt("dve", state[:], state[:], psA[:], ALU.add, reads=["psA"], writes=["state"])
                    P.copy("act", state_bf[:], state[:], reads=["state"], writes=["state_bf"])
                    P.copy("act", osb[:], pso[:].rearrange("p a b -> p (a b)"), reads=["pso"], writes=["osb"])
                    P.tt("pool", sq[:], osb[:], osb[:], ALU.mult, reads=["osb"], writes=["sq"])
                    P.op("dve", lambda e: e.tensor_reduce(out=ss[:], in_=sq[:].rearrange("p (a b) -> p a b", b=128), axis=AX.X, op=ALU.add),
                         reads=["sq"], writes=["ss"])
                    P.ts("dve", ss[:], ss[:], 1.0 / 128, LN_EPS, ALU.mult, ALU.add, writes=["ss"])
                    P.act(ss[:], ss[:], AF.Sqrt, writes=["ss"])
                    P.op("dve", lambda e: e.reciprocal(ss[:], ss[:]), writes=["ss"])
                    P.tt("dve", osb[:].rearrange("p (a b) -> p a b", b=128), osb[:].rearrange("p (a b) -> p a b", b=128),
                         ss[:].unsqueeze(2).to_broadcast([64, 8, 128]), ALU.mult, reads=["ss"], writes=["osb"])
                    P.tt("pool", osb[:], osb[:], ong[0:64, :], ALU.mult, reads=["ong"], writes=["osb"])
                    P.tt("dve", bo[:], osb[:], gs[i][:, c, :], ALU.mult, reads=["osb", ("gs", i)], writes=["bo"])
                    for h in range(8):
                        P.tr(psbT[:, h, cs], bo[:, h * 128:(h + 1) * 128], ident[0:64, 0:64], reads=["bo"], writes=["psbT"])
                P.copy("act", boT4[g4][:, :, j * 128:(j + 1) * 128], psbT[:], reads=["psbT"], writes=[("boT4", g4)])
                if j == 3 or ti == NT - 1:
                    nt = (j + 1) * 128
                    tq = t0 + 128 - nt
                    P.dma(mixT_v[:, 0:8, tq:tq + nt], aoT4[g4][:, :, 0:nt], reads=[("aoT4", g4)])
                    P.dma(mixT_v[:, 8:16, tq:tq + nt], boT4[g4][:, :, 0:nt], reads=[("boT4", g4)])
        P.flush()


N_CORES = 8
NSEQ_CORE = 16 // N_CORES
BATCH, SEQ, DEPTH, MEM_LEN = 16, 2048, 4, 256


def build_program(NSEQ=NSEQ_CORE, S=SEQ, depth=DEPTH):
    NTOK = NSEQ * S
    cfg = dict(NTOK=NTOK, S=S, NSEQ=NSEQ)
    nc = bass.Bass("TRN2", target_bir_lowering=False)

    def I(name, shape, dt=F32):
        return nc.dram_tensor(name, list(shape), dt, kind="ExternalInput").ap()

    def T(name, shape, dt):
        return nc.dram_tensor(name, list(shape), dt).ap()

    x = I("x", [NTOK, D])
    memT = I("memT", [NSEQ, D, MEM_LEN])
    ev_w_in = I("ev_w_in", [2, D, 6144])
    ev_lbT = I("ev_lbT", [2, 128, 8])
    ev_lng = I("ev_sgu_ln_g", [2, 1024])
    ev_lnb = I("ev_sgu_ln_b", [2, 1024])
    ev_wsT = I("ev_w_sT", [2, 128, 8, 128])
    ev_bs = I("ev_b_s", [2, 1024])
    ev_ong = I("ev_onorm_g", [2, 1024])
    ev_w_out = I("ev_w_out", [2, D, D])
    od_w_in = I("od_w_in_p", [2, D, 6144])
    od_cw = I("od_conv_wT", [2, D, 3])
    od_w_out = I("od_w_out", [2, D, D])
    mix_g = I("mix_ln_g", [4, D])
    mix_b = I("mix_ln_b", [4, D])
    xa_wq = I("xa_w_q", [4, D, D])
    xa_wkv = I("xa_w_kv", [4, D, 2 * D])
    xa_wo = I("xa_w_o", [4, D, D])
    xa_g = I("xa_ln_g", [4, D])
    xa_b = I("xa_ln_b", [4, D])
    pe_wq = I("peer_w_q", [4, D, D])
    pe_keysT = I("peer_keysT", [4, 128, 2, 128])
    pe_uT = I("peer_uT", [4, D, 16384])
    pe_v = I("peer_v", [4, 16384, D])
    ffn_g = I("ffn_ln_g", [4, D])
    ffn_b = I("ffn_ln_b", [4, D])
    identd = I("ident", [128, 128], BF16)
    tri128 = I("tri128", [128, 128])
    tri64 = I("tri64", [64, 64])
    out = nc.dram_tensor("out", [NTOK, D], F32, kind="ExternalOutput").ap()

    h = T("h_res", [NTOK, D], F32)
    hTs = [T("hT_a", [D, NTOK], BF16), T("hT_b", [D, NTOK], BF16)]
    mixT = T("mixT", [D, NTOK], BF16)
    kT = T("kT", [D, NSEQ * MEM_LEN], BF16)
    vv = T("vv", [NSEQ * MEM_LEN, D], BF16)
    s16 = T("s16", [NTOK, D], BF16)
    thr = T("thr", [NTOK, 16], F32)
    uTs = T("uTs", [1024, NTOK], BF16)
    gv = T("gv", [NTOK, 1024], BF16)
    qsT = T("qsT", [1024, NTOK], BF16)
    fT = T("fT", [1024, NTOK], F32)
    vi = T("vi", [NTOK, 1024], BF16)
    gs = T("gs", [NTOK, 1024], BF16)

    P = Prog(nc)
    with P.stack, nc.sbuf_tensor("ident_sb", [128, 128], BF16) as ident_t:
        ident = ident_t[:]
        P.dma(ident_t[:], identd, writes=["ident"])
        P.flush()
        cur = 0
        stage_prep(P, cfg, x, hTs[cur], ident)
        hin = x
        for layer in range(depth):
            j = layer // 2
            last = layer == depth - 1
            if layer % 2 == 0:
                stage_even_in(P, cfg, hTs[cur], ev_w_in[j], ev_lbT, j, uTs, gv, qsT, fT, vi, gs)
                stage_even_core(P, cfg, uTs, gv, qsT, fT, vi, gs, ev_lng[j], ev_lnb[j], ev_wsT[j], ev_bs[j], ev_ong[j],
                                mixT, ident, tri128, tri64)
                w_out = ev_w_out[j]
            else:
                stage_odd_in(P, cfg, hTs[cur], od_w_in[j], od_cw[j], mixT)
                w_out = od_w_out[j]
            stage_gemm_ln(P, cfg, mixT, w_out, hin, mix_g[layer], mix_b[layer], h, hTs[1 - cur], ident)
            cur = 1 - cur
            hin = h
            stage_kv(P, cfg, memT, xa_wkv[layer], kT, vv)
            stage_attn(P, cfg, hTs[cur], xa_wq[layer], kT, vv, mixT, ident)
            stage_gemm_ln(P, cfg, mixT, xa_wo[layer], h, xa_g[layer], xa_b[layer], h, hTs[1 - cur], ident)
            cur = 1 - cur
            stage_peer_q(P, cfg, hTs[cur], pe_wq[layer], pe_keysT[layer], s16, thr)
            stage_peer_ffn(P, cfg, hTs[cur], h, s16, thr, pe_uT[layer], pe_v[layer], ffn_g[layer], ffn_b[layer],
                           out if last else h, hTs[1 - cur], ident, want_hT=not last)
            cur = 1 - cur
    return nc, P


def host_layout(inp, n_cores=N_CORES, nseq=NSEQ_CORE):
    import ml_dtypes
    f = lambda a: np.ascontiguousarray(np.asarray(a, dtype=np.float32))
    shared = dict(
        ev_w_in=f(inp["ev_w_in"]),
        ev_lbT=f(np.asarray(inp["ev_lb_logits"]).reshape(2, 8, 128).transpose(0, 2, 1)),
        ev_sgu_ln_g=f(inp["ev_sgu_ln_g"]), ev_sgu_ln_b=f(inp["ev_sgu_ln_b"]),
        ev_w_sT=f(np.asarray(inp["ev_w_s"]).transpose(0, 3, 1, 2)),
        ev_b_s=f(np.asarray(inp["ev_b_s"]).reshape(2, 1024)),
        ev_onorm_g=f(inp["ev_onorm_g"]), ev_w_out=f(inp["ev_w_out"]),
        od_w_in_p=f(np.asarray(inp["od_w_in"]).reshape(2, D, 3, 16, 128).transpose(0, 1, 3, 2, 4).reshape(2, D, 6144)),
        od_conv_wT=f(np.asarray(inp["od_conv_w"]).transpose(0, 2, 1)),
        od_w_out=f(inp["od_w_out"]),
        mix_ln_g=f(inp["mix_ln_g"]), mix_ln_b=f(inp["mix_ln_b"]),
        xa_w_q=f(inp["xa_w_q"]), xa_w_kv=f(inp["xa_w_kv"]), xa_w_o=f(inp["xa_w_o"]),
        xa_ln_g=f(inp["xa_ln_g"]), xa_ln_b=f(inp["xa_ln_b"]),
        peer_w_q=f(inp["peer_w_q"]),
        peer_keysT=f(np.asarray(inp["peer_keys"]).transpose(0, 3, 1, 2)),
        peer_uT=f(np.asarray(inp["peer_u"]).transpose(0, 2, 1)),
        peer_v=f(inp["peer_v"]),
        ffn_ln_g=f(inp["ffn_ln_g"]), ffn_ln_b=f(inp["ffn_ln_b"]),
        ident=np.eye(128, dtype=np.float32).astype(ml_dtypes.bfloat16),
        tri128=np.triu(np.ones((128, 128), np.float32)),
        tri64=np.triu(np.ones((64, 64), np.float32)),
    )
    x = np.asarray(inp["x"], dtype=np.float32)
    mem = np.asarray(inp["mem"], dtype=np.float32)
    maps = []
    for c in range(n_cores):
        m = dict(shared)
        m["x"] = np.ascontiguousarray(x[c * nseq:(c + 1) * nseq].reshape(nseq * x.shape[1], D))
        m["memT"] = np.ascontiguousarray(mem[c * nseq:(c + 1) * nseq].transpose(0, 2, 1))
        maps.append(m)
    return maps


def kernel(**inputs):
    nc, _ = build_program()
    maps = host_layout(inputs)
    res = run_bass_kernel_spmd(nc, maps, core_ids=list(range(N_CORES)))
    outs = [np.asarray(r["out"]).reshape(NSEQ_CORE, SEQ, D) for r in res.results]
    return np.concatenate(outs, axis=0).astype(np.float32)
```
